# Optimizing a Trainium2 kernel written in Bass

```python
import math
import jax, jax.numpy as jnp
from jax import lax
import numpy as np

D_MODEL = 2048
BATCH = 4
SEQ = 2048
DEPTH = 4
DEC_BATCH = 128
DEC_SEQ = 1
PAST_LEN = 16384
PAGE_SIZE = 128

PLE_DIM = 256
D_FF = 5632
NORM_EPS = 1e-6
CONV_WIDTH = 4
SSM_EXPAND = 2
SSM_D_INNER = SSM_EXPAND * D_MODEL
SSM_HEAD_DIM = 64
SSM_HEADS = SSM_D_INNER // SSM_HEAD_DIM
SSM_GROUPS = 8
SSM_STATE = 128
SSM_CONV_DIM = SSM_D_INNER + 2 * SSM_GROUPS * SSM_STATE
SSM_IN_DIM = SSM_D_INNER + SSM_CONV_DIM + SSM_HEADS
SSM_CHUNK = 128
LRU_WIDTH = D_MODEL
LRU_BLOCKS = 8
LRU_BLOCK_DIM = LRU_WIDTH // LRU_BLOCKS
LRU_C = 8.0
N_SSM_LAYERS = (DEPTH + 1) // 2
N_LRU_LAYERS = DEPTH // 2

kernel_name = "hybrid_ssd_rglru_macaron_decoder_step"


def _rms_norm(x, g):
    xf = x.astype(jnp.float32)
    y = xf * lax.rsqrt(jnp.mean(xf * xf, axis=-1, keepdims=True) + NORM_EPS)
    return (y * g.astype(jnp.float32)).astype(x.dtype)


def _swiglu(h, w_gate, w_up, w_down):
    return (jax.nn.silu(h @ w_gate) * (h @ w_up)) @ w_down


def _causal_conv(u, buf, w, b):
    L = u.shape[1]
    full = jnp.concatenate([buf.astype(u.dtype), u], axis=1)
    out = b
    for k in range(CONV_WIDTH):
        out = out + full[:, k:k + L] * w[k]
    return out, full[:, L:]


def _ssd(x, dt, A, Bm, Cm, h0):
    b, l, h, p = x.shape
    g, n = Bm.shape[2], Bm.shape[3]
    j = h // g
    q = SSM_CHUNK if l % SSM_CHUNK == 0 else l
    c = l // q
    x = x.reshape(b, c, q, g, j, p)
    dt = dt.reshape(b, c, q, g, j)
    Bm = Bm.reshape(b, c, q, g, n)
    Cm = Cm.reshape(b, c, q, g, n)
    a_cum = jnp.cumsum(dt * A.reshape(g, j), axis=2)
    causal = jnp.tril(jnp.ones((q, q), dtype=bool))[:, :, None, None]
    seg = a_cum[:, :, :, None] - a_cum[:, :, None, :]
    decay = jnp.exp(jnp.where(causal, seg, -jnp.inf))
    cb = jnp.einsum('bclgn,bcsgn->bclsg', Cm, Bm)
    w_ls = cb[..., None] * decay * dt[:, :, None, :]
    y_diag = jnp.einsum('bclsgj,bcsgjp->bclgjp', w_ls, x)
    decay_to_end = jnp.exp(a_cum[:, :, -1:] - a_cum)
    chunk_states = jnp.einsum('bcsgn,bcsgj,bcsgjp->bcgjpn', Bm, decay_to_end * dt, x)
    chunk_decay = jnp.exp(a_cum[:, :, -1])

    def step(h_prev, inp):
        s_c, d_c = inp
        return h_prev * d_c[..., None, None] + s_c, h_prev

    h_last, h_in = lax.scan(step, h0.reshape(b, g, j, p, n),
                            (jnp.swapaxes(chunk_states, 0, 1), jnp.swapaxes(chunk_decay, 0, 1)))
    h_in = jnp.swapaxes(h_in, 0, 1)
    y_off = jnp.einsum('bclgn,bcgjpn,bclgj->bclgjp', Cm, h_in, jnp.exp(a_cum))
    y = (y_diag + y_off).reshape(b, l, h, p)
    return y, h_last.reshape(b, h, p, n)


def _mamba2(u, conv_buf, h0, w_in, conv_w, conv_b, dt_bias, a_log, d_skip, norm_g, w_out):
    bsz, L, _ = u.shape
    f32 = jnp.float32
    proj = u @ w_in
    z = proj[..., :SSM_D_INNER]
    xbc = proj[..., SSM_D_INNER:SSM_D_INNER + SSM_CONV_DIM]
    dt = proj[..., SSM_D_INNER + SSM_CONV_DIM:]
    xbc, new_buf = _causal_conv(xbc, conv_buf, conv_w, conv_b)
    xbc = jax.nn.silu(xbc)
    gn = SSM_GROUPS * SSM_STATE
    xs = xbc[..., :SSM_D_INNER].reshape(bsz, L, SSM_HEADS, SSM_HEAD_DIM).astype(f32)
    Bm = xbc[..., SSM_D_INNER:SSM_D_INNER + gn].reshape(bsz, L, SSM_GROUPS, SSM_STATE).astype(f32)
    Cm = xbc[..., SSM_D_INNER + gn:].reshape(bsz, L, SSM_GROUPS, SSM_STATE).astype(f32)
    dt = jax.nn.softplus(dt.astype(f32) + dt_bias.astype(f32))
    A = -jnp.exp(a_log.astype(f32))
    y, h_last = _ssd(xs, dt, A, Bm, Cm, h0.astype(f32))
    y = y + d_skip.astype(f32)[:, None] * xs
    y = y.reshape(bsz, L, SSM_D_INNER) * jax.nn.silu(z.astype(f32))
    yg = y.reshape(bsz, L, SSM_GROUPS, SSM_D_INNER // SSM_GROUPS)
    yg = yg * lax.rsqrt(jnp.mean(yg * yg, axis=-1, keepdims=True) + NORM_EPS)
    y = (yg.reshape(bsz, L, SSM_D_INNER) * norm_g.astype(f32)).astype(u.dtype)
    return y @ w_out, new_buf, h_last


def _lin_combine(left, right):
    a1, b1 = left
    a2, b2 = right
    return a1 * a2, a2 * b1 + b2


def _rg_lru(xr, h0, w_r, b_r, w_i, b_i, a_param):
    bsz, L, W = xr.shape
    f32 = jnp.float32
    xf = xr.astype(f32)
    xb = xf.reshape(bsz, L, LRU_BLOCKS, LRU_BLOCK_DIM)
    r = jax.nn.sigmoid(jnp.einsum('blnk,nkm->blnm', xb, w_r.astype(f32)).reshape(bsz, L, W) + b_r.astype(f32))
    i = jax.nn.sigmoid(jnp.einsum('blnk,nkm->blnm', xb, w_i.astype(f32)).reshape(bsz, L, W) + b_i.astype(f32))
    log_a = -LRU_C * r * jax.nn.softplus(a_param.astype(f32))
    a = jnp.exp(log_a)
    mult = jnp.sqrt(-jnp.expm1(2.0 * log_a))
    b_term = mult * i * xf
    a_cum, h = lax.associative_scan(_lin_combine, (a, b_term), axis=1)
    h = h + a_cum * h0.astype(f32)[:, None]
    return h, h[:, -1]


def _rglru_block(u, conv_buf, h0, w_x, b_x, w_y, b_y, conv_w, conv_b, w_r, b_r, w_i, b_i, a_param, w_out, b_out):
    gate = jax.nn.gelu(u @ w_y + b_y, approximate=True)
    xr = u @ w_x + b_x
    xr, new_buf = _causal_conv(xr, conv_buf, conv_w, conv_b)
    h, h_last = _rg_lru(xr, h0, w_r, b_r, w_i, b_i, a_param)
    out = (h.astype(u.dtype) * gate) @ w_out + b_out
    return out, new_buf, h_last


def _trunk(x, p, ssm_h, ssm_cb, lru_h, lru_cb, w):
    new_ssm_h, new_ssm_cb, new_lru_h, new_lru_cb = [], [], [], []
    for i in range(DEPTH):
        x = x + 0.5 * _swiglu(_rms_norm(x, w['norm_ffn_pre'][i]),
                              w['ffn_w_gate'][i, 0], w['ffn_w_up'][i, 0], w['ffn_w_down'][i, 0])
        u = _rms_norm(x, w['norm_mix'][i])
        m = i // 2
        if i % 2 == 0:
            mix, cb_new, h_new = _mamba2(u, ssm_cb[m], ssm_h[m], w['ssm_w_in'][m], w['ssm_conv_w'][m],
                                         w['ssm_conv_b'][m], w['ssm_dt_bias'][m], w['ssm_a_log'][m],
                                         w['ssm_d'][m], w['ssm_norm'][m], w['ssm_w_out'][m])
            new_ssm_h.append(h_new)
            new_ssm_cb.append(cb_new)
        else:
            mix, cb_new, h_new = _rglru_block(u, lru_cb[m], lru_h[m], w['lru_w_x'][m], w['lru_b_x'][m],
                                              w['lru_w_y'][m], w['lru_b_y'][m], w['lru_conv_w'][m],
                                              w['lru_conv_b'][m], w['lru_w_rgate'][m], w['lru_b_rgate'][m],
                                              w['lru_w_igate'][m], w['lru_b_igate'][m], w['lru_a_param'][m],
                                              w['lru_w_out'][m], w['lru_b_out'][m])
            new_lru_h.append(h_new)
            new_lru_cb.append(cb_new)
        x = x + mix
        x = x + 0.5 * _swiglu(_rms_norm(x, w['norm_ffn_post'][i]),
                              w['ffn_w_gate'][i, 1], w['ffn_w_up'][i, 1], w['ffn_w_down'][i, 1])
        gate = jax.nn.sigmoid(_rms_norm(x, w['norm_ple'][i]) @ w['ple_w_gate'][i])
        x = x + gate * (p[i] @ w['ple_w_proj'][i])
    y = _rms_norm(x, w['norm_final'])
    return y, jnp.stack(new_ssm_h), jnp.stack(new_ssm_cb), jnp.stack(new_lru_h), jnp.stack(new_lru_cb)


def setup_inputs(seed: int = 0) -> dict:
    key = jax.random.key(seed)
    keys = jax.random.split(key, 48)

    def nrm(idx, shape, scale):
        return jax.random.normal(keys[idx], shape, jnp.float32) * scale

    def gain(idx, shape):
        return 1.0 + nrm(idx, shape, 0.02)

    D, F, NM, NL = D_MODEL, D_FF, N_SSM_LAYERS, N_LRU_LAYERS
    a_pow = jax.random.uniform(keys[40], (NL, LRU_WIDTH), jnp.float32, 0.9, 0.999)
    lru_a_param = jnp.log(jnp.expm1(-jnp.log(a_pow) / LRU_C))
    dt0 = jnp.exp(jax.random.uniform(keys[41], (NM, SSM_HEADS), jnp.float32, math.log(1e-3), math.log(1e-1)))
    ssm_dt_bias = dt0 + jnp.log(-jnp.expm1(-dt0))
    ssm_a_log = jnp.log(jax.random.uniform(keys[42], (NM, SSM_HEADS), jnp.float32, 1.0, 16.0))
    return {
        'x_prompt': nrm(0, (BATCH, SEQ, D), 1.0),
        'x_sample': nrm(1, (DEC_BATCH, DEC_SEQ, D), 1.0),
        'p_prompt': nrm(2, (DEPTH, BATCH, SEQ, PLE_DIM), 1.0),
        'p_sample': nrm(3, (DEPTH, DEC_BATCH, DEC_SEQ, PLE_DIM), 1.0),
        'state_ssm': nrm(4, (NM, DEC_BATCH, SSM_HEADS, SSM_HEAD_DIM, SSM_STATE), 0.1),
        'state_ssm_conv': nrm(5, (NM, DEC_BATCH, CONV_WIDTH - 1, SSM_CONV_DIM), 1.0),
        'state_lru': nrm(6, (NL, DEC_BATCH, LRU_WIDTH), 0.5),
        'state_lru_conv': nrm(7, (NL, DEC_BATCH, CONV_WIDTH - 1, LRU_WIDTH), 1.0),
        'norm_ffn_pre': gain(8, (DEPTH, D)),
        'norm_mix': gain(9, (DEPTH, D)),
        'norm_ffn_post': gain(10, (DEPTH, D)),
        'norm_ple': gain(11, (DEPTH, D)),
        'norm_final': gain(12, (D,)),
        'ffn_w_gate': nrm(13, (DEPTH, 2, D, F), D ** -0.5),
        'ffn_w_up': nrm(14, (DEPTH, 2, D, F), D ** -0.5),
        'ffn_w_down': nrm(15, (DEPTH, 2, F, D), F ** -0.5),
        'ssm_w_in': nrm(16, (NM, D, SSM_IN_DIM), D ** -0.5),
        'ssm_conv_w': nrm(17, (NM, CONV_WIDTH, SSM_CONV_DIM), CONV_WIDTH ** -0.5),
        'ssm_conv_b': nrm(18, (NM, SSM_CONV_DIM), 0.02),
        'ssm_dt_bias': ssm_dt_bias,
        'ssm_a_log': ssm_a_log,
        'ssm_d': 1.0 + nrm(19, (NM, SSM_HEADS), 0.1),
        'ssm_norm': gain(20, (NM, SSM_D_INNER)),
        'ssm_w_out': nrm(21, (NM, SSM_D_INNER, D), SSM_D_INNER ** -0.5),
        'lru_w_x': nrm(22, (NL, D, LRU_WIDTH), D ** -0.5),
        'lru_b_x': nrm(23, (NL, LRU_WIDTH), 0.02),
        'lru_w_y': nrm(24, (NL, D, LRU_WIDTH), D ** -0.5),
        'lru_b_y': nrm(25, (NL, LRU_WIDTH), 0.02),
        'lru_conv_w': nrm(26, (NL, CONV_WIDTH, LRU_WIDTH), CONV_WIDTH ** -0.5),
        'lru_conv_b': nrm(27, (NL, LRU_WIDTH), 0.02),
        'lru_w_rgate': nrm(28, (NL, LRU_BLOCKS, LRU_BLOCK_DIM, LRU_BLOCK_DIM), LRU_BLOCK_DIM ** -0.5),
        'lru_b_rgate': nrm(29, (NL, LRU_WIDTH), 0.02),
        'lru_w_igate': nrm(30, (NL, LRU_BLOCKS, LRU_BLOCK_DIM, LRU_BLOCK_DIM), LRU_BLOCK_DIM ** -0.5),
        'lru_b_igate': nrm(31, (NL, LRU_WIDTH), 0.02),
        'lru_a_param': lru_a_param,
        'lru_w_out': nrm(32, (NL, LRU_WIDTH, D), LRU_WIDTH ** -0.5),
        'lru_b_out': nrm(33, (NL, D), 0.02),
        'ple_w_proj': nrm(34, (DEPTH, PLE_DIM, D), PLE_DIM ** -0.5),
        'ple_w_gate': nrm(35, (DEPTH, D, D), D ** -0.5),
    }


def reference(x_prompt, x_sample, p_prompt, p_sample, state_ssm, state_ssm_conv, state_lru, state_lru_conv,
              norm_ffn_pre, norm_mix, norm_ffn_post, norm_ple, norm_final,
              ffn_w_gate, ffn_w_up, ffn_w_down,
              ssm_w_in, ssm_conv_w, ssm_conv_b, ssm_dt_bias, ssm_a_log, ssm_d, ssm_norm, ssm_w_out,
              lru_w_x, lru_b_x, lru_w_y, lru_b_y, lru_conv_w, lru_conv_b,
              lru_w_rgate, lru_b_rgate, lru_w_igate, lru_b_igate, lru_a_param, lru_w_out, lru_b_out,
              ple_w_proj, ple_w_gate):
    w = {
        'norm_ffn_pre': norm_ffn_pre, 'norm_mix': norm_mix, 'norm_ffn_post': norm_ffn_post,
        'norm_ple': norm_ple, 'norm_final': norm_final,
        'ffn_w_gate': ffn_w_gate, 'ffn_w_up': ffn_w_up, 'ffn_w_down': ffn_w_down,
        'ssm_w_in': ssm_w_in, 'ssm_conv_w': ssm_conv_w, 'ssm_conv_b': ssm_conv_b,
        'ssm_dt_bias': ssm_dt_bias, 'ssm_a_log': ssm_a_log, 'ssm_d': ssm_d,
        'ssm_norm': ssm_norm, 'ssm_w_out': ssm_w_out,
        'lru_w_x': lru_w_x, 'lru_b_x': lru_b_x, 'lru_w_y': lru_w_y, 'lru_b_y': lru_b_y,
        'lru_conv_w': lru_conv_w, 'lru_conv_b': lru_conv_b,
        'lru_w_rgate': lru_w_rgate, 'lru_b_rgate': lru_b_rgate,
        'lru_w_igate': lru_w_igate, 'lru_b_igate': lru_b_igate,
        'lru_a_param': lru_a_param, 'lru_w_out': lru_w_out, 'lru_b_out': lru_b_out,
        'ple_w_proj': ple_w_proj, 'ple_w_gate': ple_w_gate,
    }
    f32 = jnp.float32
    ssm_h0 = jnp.zeros((N_SSM_LAYERS, BATCH, SSM_HEADS, SSM_HEAD_DIM, SSM_STATE), f32)
    ssm_cb0 = jnp.zeros((N_SSM_LAYERS, BATCH, CONV_WIDTH - 1, SSM_CONV_DIM), x_prompt.dtype)
    lru_h0 = jnp.zeros((N_LRU_LAYERS, BATCH, LRU_WIDTH), f32)
    lru_cb0 = jnp.zeros((N_LRU_LAYERS, BATCH, CONV_WIDTH - 1, LRU_WIDTH), x_prompt.dtype)
    y_prompt, ssm_p, ssm_conv_p, lru_p, lru_conv_p = _trunk(
        x_prompt, p_prompt, ssm_h0, ssm_cb0, lru_h0, lru_cb0, w)
    y_sample, ssm_s, ssm_conv_s, lru_s, lru_conv_s = _trunk(
        x_sample, p_sample, state_ssm, state_ssm_conv, state_lru, state_lru_conv, w)
    return (y_prompt, y_sample, ssm_p, ssm_conv_p, lru_p, lru_conv_p, ssm_s, ssm_conv_s, lru_s, lru_conv_s)
```

```python
import os
import contextlib
import numpy as np
import concourse.bass as bass
import concourse.mybir as mybir
from concourse.bass_utils import run_bass_kernel_spmd

F32 = mybir.dt.float32
BF16 = mybir.dt.bfloat16
AF = mybir.ActivationFunctionType
ALU = mybir.AluOpType
AX = mybir.AxisListType

D = 2048; FF = 5632; DEPTH = 4; PLE = 256
NTP = 1024; NTS = 16; NT = NTP + NTS
KC = D // 128
DI = 4096; NG = 8; NST = 128; CONVD = 6144; INDIM = 10304; NH = 64
EPS = 1e-6
TT = [(0, 512), (512, 512), (1024, 16)]
ENGS = ['pe', 'act', 'dve', 'pool', 'sp']
NPOOL = 12
SEMCH = 30000

CFG = {
    'layers': int(os.environ.get('MK_LAYERS', '4')),
    'ffn': int(os.environ.get('MK_FFN', '1')),
    'mixer': int(os.environ.get('MK_MIXER', '1')),
    'ple': int(os.environ.get('MK_PLE', '1')),
}


class Buf:
    __slots__ = ('w', 'r', 'name')

    def __init__(self, name=''):
        self.w = None
        self.r = {}
        self.name = name


class KB:
    def __init__(self, nc, plan=None):
        self.nc = nc
        self.dry = plan is None
        self.plan = plan
        self.ops = {e: [] for e in ENGS}
        self.nops = {e: 0 for e in ENGS}
        self.ndma = {e: 0 for e in ENGS + ['cc']}
        self.lastd = {}
        self.lastreal = {}
        self.eng = {'pe': nc.tensor, 'act': nc.scalar, 'dve': nc.vector, 'pool': nc.gpsimd, 'sp': nc.sync}
        if not self.dry:
            self.csem = {e: [nc.alloc_semaphore(f"c_{e}_{i}") for i in range(plan['ncsem'][e])] for e in ENGS}
            self.dsem = {e: [nc.alloc_semaphore(f"d_{e}_{i}") for i in range(NPOOL)] for e in ('sp', 'pool', 'cc')}

    def op(self, eng, fn, reads=(), writes=(), dma=False, extra=()):
        i = self.nops[eng]
        self.nops[eng] = i + 1
        if fn is not None and not dma:
            self.lastreal[eng] = i
        if dma:
            q = 'cc' if dma == 'cc' else eng
            inc = 1 if dma == 'cc' else 16
            n = self.ndma[q]
            self.ndma[q] = n + 1
            k = n % NPOOL
            val = inc * (n // NPOOL + 1)
            tok = ('d', q, k, val)
        else:
            tok = ('c', eng, i)
        if self.dry:
            deps = set(extra)
            for b in reads:
                if b.w is not None:
                    deps.add(b.w)
            for b in writes:
                if b.w is not None:
                    deps.add(b.w)
                deps.update(b.r.values())
            if dma and val > inc:
                deps.add(('d', q, k, val - inc))
            self.ops[eng].append(deps)
            for b in reads:
                key = (tok[0], tok[1]) if tok[0] == 'c' else tok
                b.r[key] = tok
            for b in writes:
                b.w = tok
                b.r = {}
            if dma:
                self.lastd[(q, k)] = tok
            return tok
        waits, sig = self.plan['ops'][eng][i]
        e = self.eng[eng]
        for (kind, a, b, v) in waits:
            if kind == 'c':
                e.wait_ge(self.csem[a][b], v)
            else:
                e.wait_ge(self.dsem[a][b], v)
        if fn is not None:
            inst = fn(e)
            if dma == 'cc':
                inst.then_inc(self.dsem['cc'][k])
            elif dma:
                inst.then_inc(self.dsem[eng][k], 16)
            elif sig is not None:
                inst.then_inc(self.csem[eng][sig], 1)
        else:
            assert sig is None
        return tok

    def last_tokens(self):
        toks = []
        for e in ENGS:
            if e in self.lastreal:
                toks.append(('c', e, self.lastreal[e]))
        toks.extend(self.lastd.values())
        return toks

    def barrier(self):
        if self.dry:
            toks = self.last_tokens()
        else:
            toks = ()
        for e in ENGS:
            self.op(e, None, extra=[t for t in toks if not (t[0] == 'c' and t[1] == e)])

    def finish(self):
        toks = list(self.lastd.values()) if self.dry else ()
        self.op('sp', None, extra=toks)

    def resolve(self):
        need = {e: set() for e in ENGS}
        waits_all = {e: [] for e in ENGS}
        for e in ENGS:
            seen = {}
            for i, deps in enumerate(self.ops[e]):
                ws = []
                for t in sorted(deps, key=lambda t: (t[0], t[1], t[2], t[3] if len(t) > 3 else 0)):
                    if t[0] == 'c':
                        f, j = t[1], t[2]
                        if f == e and (e in ('pe', 'sp') or j >= i):
                            continue
                        if seen.get(('c', f), -1) >= j:
                            continue
                        seen[('c', f)] = j
                        need[f].add(j)
                        ws.append(t)
                    else:
                        key = ('d', t[1], t[2])
                        if seen.get(key, 0) >= t[3]:
                            continue
                        seen[key] = t[3]
                        ws.append(t)
                waits_all[e].append(ws)
        signo = {}
        ncsem = {}
        for e in ENGS:
            n = 0
            for j in sorted(need[e]):
                n += 1
                signo[(e, j)] = n
            ncsem[e] = max(1, (n + SEMCH - 1) // SEMCH)
        plan_ops = {}
        for e in ENGS:
            lst = []
            for i, ws in enumerate(waits_all[e]):
                out = []
                for t in ws:
                    if t[0] == 'c':
                        n = signo[(t[1], t[2])]
                        out.append(('c', t[1], (n - 1) // SEMCH, (n - 1) % SEMCH + 1))
                    else:
                        out.append(('d', t[1], t[2], t[3]))
                sig = None
                if (e, i) in signo:
                    sig = (signo[(e, i)] - 1) // SEMCH
                lst.append((out, sig))
            plan_ops[e] = lst
        return {'ops': plan_ops, 'ncsem': ncsem}


class WStream:
    def __init__(self, k, slots, sbufs):
        self.k = k
        self.slots = slots
        self.sbufs = sbufs
        self.free = list(range(len(slots)))
        self.pending = []
        self.issued = []
        self.pi = 0

    def plan_loads(self, loads):
        self.pending.extend(loads)
        self.pump()

    def pump(self):
        while self.free and self.pi < len(self.pending):
            key, parts = self.pending[self.pi]
            self.pi += 1
            s = self.free.pop(0)
            tile = self.slots[s]
            for (dst_fn, src, srcb) in parts:
                dst = dst_fn(tile)
                self.k.op('pool', lambda e, dst=dst, src=src: e.dma_start(out=dst, in_=src), reads=[srcb], writes=[self.sbufs[s]], dma=True)
            self.issued.append((key, s))

    def get(self, key):
        assert self.issued, f"no issued load for {key}"
        k0, s = self.issued.pop(0)
        assert k0 == key, (k0, key)
        return s, self.slots[s], self.sbufs[s]

    def release(self, s):
        self.free.append(s)
        self.pump()


WSPEC = {
    'ffn_w_gate': (4, 2 * D, FF), 'ffn_w_up': (4, 2 * D, FF), 'ffn_w_down': (4, 2 * FF, D),
    'ple_w_gate': (4, D, D), 'ple_w_proj': (4, PLE, D),
    'ssm_w_in': (2, D, INDIM), 'ssm_w_out': (2, DI, D),
    'lru_w_x': (2, D, D), 'lru_w_y': (2, D, D), 'lru_w_out': (2, D, D),
    'lru_w_rgate': (2, D, 256), 'lru_w_igate': (2, D, 256),
}
NSV = 64 * 3 + 48 * 5 + DI
NLV = 11


def build(plan=None):
    nc = bass.Bass("TRN2", target_bir_lowering=False)
    k = KB(nc, plan)
    L = CFG['layers']

    def din(name, shape):
        return nc.dram_tensor(name, list(shape), F32, kind="ExternalInput").ap()

    def dout(name, shape):
        return nc.dram_tensor(name, list(shape), F32, kind="ExternalOutput").ap()

    xT_d = din("xT", [D, NT])
    pT_d = din("pT", [DEPTH, PLE, NT])
    consts_d = din("consts", [128, 4 * 128])
    gains_d = din("gains", [128, (4 * DEPTH + 1) * KC])
    wsh_d = {n: din(n + "_full", [nl, rows, cols]) for n, (nl, rows, cols) in WSPEC.items()}
    ssmvec_d = din("ssmvec", [2, 128, NSV])
    lruvec_d = din("lruvec", [2, 128, NLV * KC])
    st_ssm_d = din("st_ssm", [2, NTS, NH, 64, NST])
    st_ssm_convT_d = din("st_ssm_convT", [2, 128, 48, 3, NTS])
    st_lruT_d = din("st_lruT", [2, 128, KC, NTS])
    st_lru_convT_d = din("st_lru_convT", [2, 128, KC, 3, NTS])
    yT_d = dout("yT", [D, NT])
    o_ssm_p = dout("o_ssm_p", [2, NG, 128, 512])
    o_ssm_conv_p = dout("o_ssm_conv_p", [2, 128, 144])
    o_lru_p = dout("o_lru_p", [2, 128, KC])
    o_lru_conv_p = dout("o_lru_conv_p", [2, 128, KC * 3])
    o_ssm_s = dout("o_ssm_s", [2, NTS, NH, 64, NST])
    o_ssm_conv_sT = dout("o_ssm_conv_sT", [2, 128, 48, 3, NTS])
    o_lru_sT = dout("o_lru_sT", [2, 128, KC, NTS])
    o_lru_conv_sT = dout("o_lru_conv_sT", [2, 128, KC, 3, NTS])

    es = contextlib.ExitStack()
    uid = [0]

    def pt(ph, name, shape, dt=F32):
        uid[0] += 1
        return ph.enter_context(nc.sbuf_tensor(f"{name}_{uid[0]}", list(shape), dt))

    def sb(name, shape, dt=F32):
        return es.enter_context(nc.sbuf_tensor(name, list(shape), dt))

    def mm(ps, lhsT, rhs, st, sp, R, W):
        k.op('pe', lambda e: e.matmul(ps, lhsT=lhsT, rhs=rhs, start=st, stop=sp), reads=R, writes=W)

    def tr(ps, in_, idn, R, W):
        k.op('pe', lambda e: e.transpose(ps, in_, idn), reads=R, writes=W)

    def act(out, in_, func, R, W, **kw):
        k.op('act', lambda e: e.activation(out=out, in_=in_, func=func, **kw), reads=R, writes=W)

    def tt(out, in0, in1, op, R, W, eng='dve'):
        k.op(eng, lambda e: e.tensor_tensor(out=out, in0=in0, in1=in1, op=op), reads=R, writes=W)

    def ts(out, in0, s1, s2, op0, op1, R, W):
        if s2 is None:
            k.op('dve', lambda e: e.tensor_scalar(out=out, in0=in0, scalar1=s1, scalar2=None, op0=op0), reads=R, writes=W)
        else:
            k.op('dve', lambda e: e.tensor_scalar(out=out, in0=in0, scalar1=s1, scalar2=s2, op0=op0, op1=op1), reads=R, writes=W)

    def stt(out, in0, scalar, in1, op0, op1, R, W):
        k.op('dve', lambda e: e.scalar_tensor_tensor(out=out, in0=in0, scalar=scalar, in1=in1, op0=op0, op1=op1), reads=R, writes=W)

    def cp(out, in_, R, W, eng='dve'):
        if eng == 'act':
            k.op('act', lambda e: e.activation(out=out, in_=in_, func=AF.Copy), reads=R, writes=W)
        else:
            k.op(eng, lambda e: e.tensor_copy(out=out, in_=in_), reads=R, writes=W)

    def dma(out, in_, R, W, q='sp'):
        k.op(q, lambda e: e.dma_start(out=out, in_=in_), reads=R, writes=W, dma=True)

    xT = sb("xT_sb", [128, KC, NT]); xT_b = [Buf(f"x{c}") for c in range(KC)]
    hT = sb("hT_sb", [128, KC, NT], BF16); hT_b = Buf("hT")
    consts = sb("consts_sb", [128, 512]); consts_b = Buf("consts")
    gains = sb("gains_sb", [128, (4 * DEPTH + 1) * KC]); gains_b = Buf("gains")
    ones_bf = sb("ones_bf", [128, 128], BF16); ones_b = Buf("ones")
    ones_f = sb("ones_f", [128, 128], F32)
    ident_bf = sb("ident_bf", [128, 128], BF16); identbf_b = Buf("identbf")
    NS = 5
    wslots = [sb(f"wslot{i}", [128, 4096], BF16) for i in range(NS)]
    wbufs = [Buf(f"w{i}") for i in range(NS)]
    ws = WStream(k, wslots, wbufs)
    PS = [es.enter_context(nc.psum_tensor(f"ps{i}", [128, 512], F32)) for i in range(8)]
    PSb = [Buf(f"ps{i}") for i in range(8)]
    ident = consts[:, 0:128]
    Umat = consts[:, 128:256]
    isB = consts[:, 256:257]

    for c in range(KC):
        dma(xT[:, c, :], xT_d[c * 128:(c + 1) * 128, :], [], [xT_b[c]])
    dma(consts[:], consts_d, [], [consts_b])
    dma(gains[:], gains_d, [], [gains_b])
    k.op('dve', lambda e: e.memset(ones_bf[:], 1.0), writes=[ones_b])
    k.op('dve', lambda e: e.memset(ones_f[:], 1.0), writes=[ones_b])
    cp(ident_bf[:], ident, [consts_b], [identbf_b])

    wf = {n: {} for n in WSPEC}

    def gather(name, l):
        wf[name][l] = (wsh_d[name][l], Buf())

    def gather_layer(layer):
        if layer >= L:
            return
        if CFG['ffn']:
            for n in ('ffn_w_gate', 'ffn_w_up', 'ffn_w_down'):
                gather(n, layer)
        if CFG['mixer']:
            m = layer // 2
            if layer % 2 == 0:
                for n in ('ssm_w_in', 'ssm_w_out'):
                    gather(n, m)
            else:
                for n in ('lru_w_x', 'lru_w_rgate', 'lru_w_igate', 'lru_w_y', 'lru_w_out'):
                    gather(n, m)
        if CFG['ple']:
            for n in ('ple_w_gate', 'ple_w_proj'):
                gather(n, layer)

    xcount = [0]

    def pair_exchange(src, F, R, dst, dst_b):
        xcount[0] += 1
        ci = nc.dram_tensor(f"xch_i{xcount[0]}", [128, F], F32)
        co = nc.dram_tensor(f"xch_o{xcount[0]}", [256, F], F32)
        cib, cob = Buf(), Buf()
        dma(ci.ap(), src, R, [cib])
        k.op('pool', lambda e: e.collective_compute("AllGather", ALU.bypass, replica_groups=[[0, 1], [2, 3], [4, 5], [6, 7]],
                                                    ins=[ci.ap().opt()], outs=[co.ap().opt()]), reads=[cib], writes=[cob], dma='cc')
        dma(dst, co.ap()[0:128, :], [cob], [dst_b])
        ts(dst, dst, isB, None, ALU.mult, None, [dst_b, consts_b], [dst_b])

    def gain_col(kind, layer, c):
        j = (kind * DEPTH + layer) * KC + c
        return gains[:, j:j + 1]

    def rmsnorm(kind, layer, out_tile, out_buf):
        with contextlib.ExitStack() as ph:
            sq = [pt(ph, f"nsq{i}", [128, 512], BF16) for i in range(2)]
            sq_b = [Buf(), Buf()]
            rstd = pt(ph, "nrstd", [128, 512], F32); rstd_b = Buf()
            for ti, (t0, n) in enumerate(TT):
                ps, psb = PS[6 + (ti % 2)], PSb[6 + (ti % 2)]
                for c in range(KC):
                    s, sbf = sq[c % 2], sq_b[c % 2]
                    act(s[:, :n], xT[:, c, t0:t0 + n], AF.Square, [xT_b[c]], [sbf])
                    mm(ps[:, :n], ones_bf[:], s[:, :n], c == 0, c == KC - 1, [sbf, ones_b], [psb])
                act(rstd[:, :n], ps[:, :n], AF.Sqrt, [psb], [rstd_b], scale=1.0 / D, bias=EPS)
                k.op('dve', lambda e: e.reciprocal(out=rstd[:, :n], in_=rstd[:, :n]), reads=[rstd_b], writes=[rstd_b])
                for c in range(KC):
                    wr = [out_buf] if not isinstance(out_buf, list) else [out_buf[c]]
                    stt(out_tile[:, c, t0:t0 + n], xT[:, c, t0:t0 + n], gain_col(kind, layer, c), rstd[:, :n], ALU.mult, ALU.mult,
                        [xT_b[c], rstd_b, gains_b], wr)
            k.barrier()

    def wview(t, kc, f):
        return t[:, 0:kc * f].rearrange("p (kc f) -> p kc f", kc=kc)

    def wsrc(full, r0, nr, c0, ncol):
        return full[r0:r0 + nr, c0:c0 + ncol].rearrange("(kc p) f -> p kc f", p=128)

    G = 2
    NGRP = FF // (128 * G)

    def ffn_loads(layer, half):
        loads = []
        GD = 2
        pend = []
        for g in range(NGRP):
            f0 = g * 128 * G
            for nm, wn in (('g', 'ffn_w_gate'), ('u', 'ffn_w_up')):
                full, fb = wf[wn][layer]
                loads.append(((nm, layer, half, g), [(lambda t: wview(t, KC, 128 * G), wsrc(full, half * D, D, f0, 128 * G), fb)]))
            full, fb = wf['ffn_w_down'][layer]
            pend.append((('d', layer, half, g), [(lambda t: wview(t, G, D), wsrc(full, half * FF + f0, 128 * G, 0, D), fb)]))
            if g % GD == GD - 1:
                loads += pend
                pend = []
        return loads

    def ffn(layer, half):
        GD = 2
        with contextlib.ExitStack() as ph:
            actt = pt(ph, "ffn_act", [128, G * GD, NT], BF16); act_b = [Buf() for _ in range(G * GD)]
            sg = [pt(ph, f"ffn_sg{i}", [128, 512], F32) for i in range(2)]
            sg_b = [Buf(), Buf()]
            cnt = 0
            cntd = 0
            for g in range(NGRP):
                sub = g % GD
                sgi, wg_t, wg_b = ws.get(('g', layer, half, g))
                sui, wu_t, wu_b = ws.get(('u', layer, half, g))
                wg = wview(wg_t, KC, 128 * G); wu = wview(wu_t, KC, 128 * G)
                for fc in range(G):
                    af = sub * G + fc
                    for (t0, n) in TT:
                        pg, pgb = PS[cnt % 2], PSb[cnt % 2]
                        pu, pub = PS[2 + cnt % 2], PSb[2 + cnt % 2]
                        s, s_b = sg[cnt % 2], sg_b[cnt % 2]
                        cnt += 1
                        for kc in range(KC):
                            mm(pg[:, :n], wg[:, kc, fc * 128:(fc + 1) * 128], hT[:, kc, t0:t0 + n], kc == 0, kc == KC - 1, [wg_b, hT_b], [pgb])
                        for kc in range(KC):
                            mm(pu[:, :n], wu[:, kc, fc * 128:(fc + 1) * 128], hT[:, kc, t0:t0 + n], kc == 0, kc == KC - 1, [wu_b, hT_b], [pub])
                        act(s[:, :n], pg[:, :n], AF.Silu, [pgb], [s_b])
                        tt(actt[:, af, t0:t0 + n], s[:, :n], pu[:, :n], ALU.mult, [s_b, pub], [act_b[af]])
                ws.release(sgi); ws.release(sui)
                if sub != GD - 1:
                    continue
                wds = []
                for gg in range(g - GD + 1, g + 1):
                    sdi, wd_t, wd_b = ws.get(('d', layer, half, gg))
                    wds.append((sdi, wview(wd_t, G, D), wd_b))
                nf = G * GD
                for dc in range(KC):
                    for (t0, n) in TT:
                        pd, pdb = PS[4 + cntd % 2], PSb[4 + cntd % 2]
                        cntd += 1
                        for af in range(nf):
                            _, wdn, wd_b = wds[af // G]
                            mm(pd[:, :n], wdn[:, af % G, dc * 128:(dc + 1) * 128], actt[:, af, t0:t0 + n], af == 0, af == nf - 1, [wd_b, act_b[af]], [pdb])
                        stt(xT[:, dc, t0:t0 + n], pd[:, :n], 0.5, xT[:, dc, t0:t0 + n], ALU.mult, ALU.add, [pdb, xT_b[dc]], [xT_b[dc]])
                for (sdi, _, _) in wds:
                    ws.release(sdi)
            k.barrier()

    def ple_loads(layer):
        loads = []
        full, fb = wf['ple_w_gate'][layer]
        for dg in range(D // 256):
            loads.append((('pg', layer, dg), [(lambda t: wview(t, KC, 256), wsrc(full, 0, D, dg * 256, 256), fb)]))
        full, fb = wf['ple_w_proj'][layer]
        loads.append((('pp', layer), [(lambda t: wview(t, 2, D), wsrc(full, 0, PLE, 0, D), fb)]))
        return loads

    def ple(layer):
        with contextlib.ExitStack() as ph:
            pf = pt(ph, "ple_pf", [128, 2, NT], F32); pf_b = Buf()
            pb = pt(ph, "ple_pb", [128, 2, NT], BF16); pb_b = Buf()
            gs = pt(ph, "ple_gs", [128, KC, NT], BF16); gs_b = [Buf() for _ in range(KC)]
            tmp = [pt(ph, f"ple_tmp{i}", [128, 512], F32) for i in range(2)]
            tmp_b = [Buf(), Buf()]
            for kc in range(2):
                dma(pf[:, kc, :], pT_d[layer, kc * 128:(kc + 1) * 128, :], [], [pf_b])
            act(pb[:], pf[:], AF.Copy, [pf_b], [pb_b])
            cnt = 0
            for dg in range(D // 256):
                si, wt, wb = ws.get(('pg', layer, dg))
                wg = wview(wt, KC, 256)
                for j in range(2):
                    dc = dg * 2 + j
                    for (t0, n) in TT:
                        pg, pgb = PS[cnt % 2], PSb[cnt % 2]
                        cnt += 1
                        for kc in range(KC):
                            mm(pg[:, :n], wg[:, kc, j * 128:(j + 1) * 128], hT[:, kc, t0:t0 + n], kc == 0, kc == KC - 1, [wb, hT_b], [pgb])
                        act(gs[:, dc, t0:t0 + n], pg[:, :n], AF.Sigmoid, [pgb], [gs_b[dc]])
                ws.release(si)
            si, wt, wb = ws.get(('pp', layer))
            wp = wview(wt, 2, D)
            for dc in range(KC):
                for (t0, n) in TT:
                    pp, ppb = PS[2 + cnt % 2], PSb[2 + cnt % 2]
                    t, t_b = tmp[cnt % 2], tmp_b[cnt % 2]
                    cnt += 1
                    for kc in range(2):
                        mm(pp[:, :n], wp[:, kc, dc * 128:(dc + 1) * 128], pb[:, kc, t0:t0 + n], kc == 0, kc == 1, [wb, pb_b], [ppb])
                    tt(t[:, :n], gs[:, dc, t0:t0 + n], pp[:, :n], ALU.mult, [gs_b[dc], ppb], [t_b])
                    tt(xT[:, dc, t0:t0 + n], xT[:, dc, t0:t0 + n], t[:, :n], ALU.add, [t_b, xT_b[dc]], [xT_b[dc]])
            ws.release(si)
            k.barrier()

    def lru_loads(layer):
        m = layer // 2
        loads = []
        fx, fxb = wf['lru_w_x'][m]; fy, fyb = wf['lru_w_y'][m]; fo, fob = wf['lru_w_out'][m]
        fr, frb = wf['lru_w_rgate'][m]; fi, fib = wf['lru_w_igate'][m]
        for nb in range(8):
            loads.append((('lt', m, nb), [(lambda t: wview(t, KC, 256), wsrc(fx, 0, D, nb * 256, 256), fxb)]))
        for nb in range(8):
            loads.append((('lx', m, nb), [(lambda t: wview(t, KC, 256), wsrc(fx, 0, D, nb * 256, 256), fxb)]))
            loads.append((('lg', m, nb), [(lambda t: wview(t, 4, 256)[:, 0:2, :], wsrc(fr, nb * 256, 256, 0, 256), frb),
                                         (lambda t: wview(t, 4, 256)[:, 2:4, :], wsrc(fi, nb * 256, 256, 0, 256), fib)]))
            loads.append((('ly', m, nb), [(lambda t: wview(t, KC, 256), wsrc(fy, 0, D, nb * 256, 256), fyb)]))
            loads.append((('lo', m, nb), [(lambda t: wview(t, 2, D), wsrc(fo, nb * 256, 256, 0, D), fob)]))
        return loads

    def lru(layer):
        m = layer // 2
        with contextlib.ExitStack() as ph:
            lv = pt(ph, "lru_lv", [128, NLV * KC]); lv_b = Buf()
            dma(lv[:], lruvec_d[m], [], [lv_b])

            def V(kind, c):
                return lv[:, kind * KC + c:kind * KC + c + 1]
            spn = pt(ph, "lru_sp", [128, 2 * KC]); spn_b = Buf()
            act(spn[:, 0:KC], lv[:, 9 * KC:10 * KC], AF.Exp, [lv_b], [spn_b])
            act(spn[:, 0:KC], spn[:, 0:KC], AF.Ln, [spn_b], [spn_b], bias=1.0)
            ts(spn[:, KC:2 * KC], spn[:, 0:KC], -16.0, None, ALU.mult, None, [spn_b], [spn_b])
            ts(spn[:, 0:KC], spn[:, 0:KC], -8.0, None, ALU.mult, None, [spn_b], [spn_b])
            tail = pt(ph, "lru_tail", [128, KC * 3]); tail_b = Buf()
            hist = pt(ph, "lru_hist", [128, KC * 3]); hist_b = Buf()
            tp, tpb = PS[5], PSb[5]
            for nb in range(8):
                si, wt, wb = ws.get(('lt', m, nb))
                w = wview(wt, KC, 256)
                for j in range(2):
                    c = nb * 2 + j
                    for kc in range(KC):
                        mm(tp[:, c * 3:c * 3 + 3], w[:, kc, j * 128:(j + 1) * 128], hT[:, kc, NTP - 3:NTP], kc == 0, kc == KC - 1, [wb, hT_b], [tpb])
                ws.release(si)
            tt(tail[:].rearrange("p (c k) -> p c k", k=3), tp[:, 0:KC * 3].rearrange("p (c k) -> p c k", k=3),
               lv[:, 0:KC].unsqueeze(2).to_broadcast([128, KC, 3]), ALU.add, [tpb, lv_b], [tail_b])
            dma(o_lru_conv_p[m], tail[:], [tail_b], [])
            pair_exchange(tail[:], KC * 3, [tail_b], hist[:], hist_b)
            cin = pt(ph, "lru_cin", [128, NTP + 3]); cin_b = Buf()
            xc = pt(ph, "lru_xc", [128, 2, NT]); xc_b = Buf()
            xcb = pt(ph, "lru_xcb", [128, 2, NT], BF16); xcb_b = Buf()
            aa = pt(ph, "lru_a", [128, 2, NT]); aa_b = Buf()
            bbt = pt(ph, "lru_b", [128, 2, NT]); bb_b = Buf()
            hp = pt(ph, "lru_h", [128, 2, NT]); hp_b = Buf()
            hg = pt(ph, "lru_hg", [128, 2, NT], BF16); hg_b = Buf()
            t1 = pt(ph, "lru_t1", [128, 512]); t1_b = Buf()
            t2 = pt(ph, "lru_t2", [128, 512]); t2_b = Buf()
            t3 = pt(ph, "lru_t3", [128, 512]); t3_b = Buf()
            hs = pt(ph, "lru_hs", [128, 2, 3, NTS]); hs_b = Buf()
            raws = pt(ph, "lru_raws", [128, 2, NTS]); raws_b = Buf()
            h0 = pt(ph, "lru_h0", [128, 2, NTS]); h0_b = Buf()
            fin = pt(ph, "lru_fin", [128, 2]); fin_b = Buf()
            prev = pt(ph, "lru_prev", [128, 2]); prev_b = Buf()
            fin2 = pt(ph, "lru_fin2", [128, 2]); fin2_b = Buf()
            cnt = 0
            for nb in range(8):
                c0 = nb * 2
                dma(hs[:], st_lru_convT_d[m][:, c0:c0 + 2], [], [hs_b])
                dma(h0[:], st_lruT_d[m][:, c0:c0 + 2], [], [h0_b])
                si, wt, wb = ws.get(('lx', m, nb))
                w = wview(wt, KC, 256)
                for j in range(2):
                    c = c0 + j
                    for (t0, n) in TT:
                        ps, psb = PS[cnt % 2], PSb[cnt % 2]
                        cnt += 1
                        for kc in range(KC):
                            mm(ps[:, :n], w[:, kc, j * 128:(j + 1) * 128], hT[:, kc, t0:t0 + n], kc == 0, kc == KC - 1, [wb, hT_b], [psb])
                        if t0 < NTP:
                            act(cin[:, 3 + t0:3 + t0 + n], ps[:, :n], AF.Identity, [psb, lv_b], [cin_b], bias=V(0, c), scale=1.0)
                        else:
                            act(raws[:, j, :], ps[:, :n], AF.Identity, [psb, lv_b], [raws_b], bias=V(0, c), scale=1.0)
                    cp(cin[:, 0:3], hist[:, c * 3:c * 3 + 3], [hist_b], [cin_b])
                    ts(xc[:, j, 0:NTP], cin[:, 0:NTP], V(2, c), V(6, c), ALU.mult, ALU.add, [cin_b, lv_b], [xc_b])
                    for kk in range(1, 4):
                        stt(xc[:, j, 0:NTP], cin[:, kk:kk + NTP], V(2 + kk, c), xc[:, j, 0:NTP], ALU.mult, ALU.add, [cin_b, xc_b, lv_b], [xc_b])
                    ts(xc[:, j, NTP:NT], hs[:, j, 0, :], V(2, c), V(6, c), ALU.mult, ALU.add, [hs_b, lv_b], [xc_b])
                    for kk in range(1, 3):
                        stt(xc[:, j, NTP:NT], hs[:, j, kk, :], V(2 + kk, c), xc[:, j, NTP:NT], ALU.mult, ALU.add, [hs_b, xc_b, lv_b], [xc_b])
                    stt(xc[:, j, NTP:NT], raws[:, j, :], V(5, c), xc[:, j, NTP:NT], ALU.mult, ALU.add, [raws_b, xc_b, lv_b], [xc_b])
                ws.release(si)
                dma(o_lru_conv_sT[m][:, c0:c0 + 2, 0:2, :], hs[:, :, 1:3, :], [hs_b], [])
                dma(o_lru_conv_sT[m][:, c0:c0 + 2, 2, :], raws[:], [raws_b], [])
                act(xcb[:], xc[:], AF.Copy, [xc_b], [xcb_b])
                si, wt, wb = ws.get(('lg', m, nb))
                wg = wview(wt, 4, 256)
                for mo in range(2):
                    c = c0 + mo
                    for (t0, n) in TT:
                        pr, prb = PS[cnt % 2], PSb[cnt % 2]
                        pi, pib = PS[2 + cnt % 2], PSb[2 + cnt % 2]
                        cnt += 1
                        for kc in range(2):
                            mm(pr[:, :n], wg[:, kc, mo * 128:(mo + 1) * 128], xcb[:, kc, t0:t0 + n], kc == 0, kc == 1, [wb, xcb_b], [prb])
                        for kc in range(2):
                            mm(pi[:, :n], wg[:, 2 + kc, mo * 128:(mo + 1) * 128], xcb[:, kc, t0:t0 + n], kc == 0, kc == 1, [wb, xcb_b], [pib])
                        act(t1[:, :n], pr[:, :n], AF.Sigmoid, [prb, lv_b], [t1_b], bias=V(7, c), scale=1.0)
                        act(aa[:, mo, t0:t0 + n], t1[:, :n], AF.Exp, [t1_b, spn_b], [aa_b], scale=spn[:, c:c + 1])
                        act(t2[:, :n], t1[:, :n], AF.Exp, [t1_b, spn_b], [t2_b], scale=spn[:, KC + c:KC + c + 1])
                        act(t2[:, :n], t2[:, :n], AF.Sqrt, [t2_b], [t2_b], scale=-1.0, bias=1.0)
                        act(t3[:, :n], pi[:, :n], AF.Sigmoid, [pib, lv_b], [t3_b], bias=V(8, c), scale=1.0)
                        tt(t2[:, :n], t2[:, :n], t3[:, :n], ALU.mult, [t2_b, t3_b], [t2_b])
                        tt(bbt[:, mo, t0:t0 + n], t2[:, :n], xc[:, mo, t0:t0 + n], ALU.mult, [t2_b, xc_b], [bb_b])
                ws.release(si)
                si, wt, wb = ws.get(('ly', m, nb))
                w = wview(wt, KC, 256)
                for j in range(2):
                    c = c0 + j
                    for (t0, n) in TT:
                        ps, psb = PS[cnt % 2], PSb[cnt % 2]
                        cnt += 1
                        for kc in range(KC):
                            mm(ps[:, :n], w[:, kc, j * 128:(j + 1) * 128], hT[:, kc, t0:t0 + n], kc == 0, kc == KC - 1, [wb, hT_b], [psb])
                        act(hg[:, j, t0:t0 + n], ps[:, :n], AF.Gelu_apprx_tanh, [psb, lv_b], [hg_b], bias=V(1, c), scale=1.0)
                ws.release(si)
                for mo in range(2):
                    k.op('dve', lambda e: e.tensor_tensor_scan(out=hp[:, mo, 0:NTP], data0=aa[:, mo, 0:NTP], data1=bbt[:, mo, 0:NTP], initial=0.0, op0=ALU.mult, op1=ALU.add),
                         reads=[aa_b, bb_b], writes=[hp_b])
                cp(fin[:], hp[:, :, NTP - 1], [hp_b], [fin_b])
                pair_exchange(fin[:], 2, [fin_b], prev[:], prev_b)
                for mo in range(2):
                    k.op('dve', lambda e: e.tensor_tensor_scan(out=hp[:, mo, 0:NTP], data0=aa[:, mo, 0:NTP], data1=bbt[:, mo, 0:NTP], initial=prev[:, mo:mo + 1], op0=ALU.mult, op1=ALU.add),
                         reads=[aa_b, bb_b, prev_b], writes=[hp_b])
                tt(hp[:, :, NTP:NT], aa[:, :, NTP:NT], h0[:], ALU.mult, [aa_b, h0_b], [hp_b])
                tt(hp[:, :, NTP:NT], hp[:, :, NTP:NT], bbt[:, :, NTP:NT], ALU.add, [hp_b, bb_b], [hp_b])
                cp(fin2[:], hp[:, :, NTP - 1], [hp_b], [fin2_b])
                dma(o_lru_p[m][:, c0:c0 + 2], fin2[:], [fin2_b], [])
                dma(o_lru_sT[m][:, c0:c0 + 2, :], hp[:, :, NTP:NT], [hp_b], [])
                for j in range(2):
                    tt(hg[:, j, :], hg[:, j, :], hp[:, j, :], ALU.mult, [hg_b, hp_b], [hg_b])
                si, wt, wb = ws.get(('lo', m, nb))
                wo = wview(wt, 2, D)
                for dc in range(KC):
                    for (t0, n) in TT:
                        po, pob = PS[4 + cnt % 2], PSb[4 + cnt % 2]
                        cnt += 1
                        for rc in range(2):
                            mm(po[:, :n], wo[:, rc, dc * 128:(dc + 1) * 128], hg[:, rc, t0:t0 + n], rc == 0, rc == 1, [wb, hg_b], [pob])
                        if nb == 0:
                            stt(xT[:, dc, t0:t0 + n], po[:, :n], V(10, dc), xT[:, dc, t0:t0 + n], ALU.add, ALU.add, [pob, xT_b[dc], lv_b], [xT_b[dc]])
                        else:
                            tt(xT[:, dc, t0:t0 + n], xT[:, dc, t0:t0 + n], po[:, :n], ALU.add, [pob, xT_b[dc]], [xT_b[dc]])
                ws.release(si)
            k.barrier()

    def mamba_loads(layer):
        m = layer // 2
        fi, fib = wf['ssm_w_in'][m]; fo, fob = wf['ssm_w_out'][m]
        loads = []
        for cg in range(24):
            loads.append((('mt', m, cg), [(lambda t: wview(t, KC, 256), wsrc(fi, 0, D, DI + cg * 256, 256), fib)]))
        loads.append((('mdt', m), [(lambda t: wview(t, KC, 64), wsrc(fi, 0, D, DI + CONVD, 64), fib)]))
        for g in range(NG):
            for j in range(2):
                loads.append((('mx', m, g, j), [(lambda t: wview(t, KC, 256), wsrc(fi, 0, D, DI + g * 512 + j * 256, 256), fib)]))
            loads.append((('mbc', m, g), [(lambda t: wview(t, KC, 256)[:, :, 0:128], wsrc(fi, 0, D, 2 * DI + g * 128, 128), fib),
                                          (lambda t: wview(t, KC, 256)[:, :, 128:256], wsrc(fi, 0, D, 2 * DI + 1024 + g * 128, 128), fib)]))
            for j in range(2):
                loads.append((('mz', m, g, j), [(lambda t: wview(t, KC, 256), wsrc(fi, 0, D, g * 512 + j * 256, 256), fib)]))
            for j in range(2):
                loads.append((('mo', m, g, j), [(lambda t: wview(t, 2, D), wsrc(fo, g * 512 + j * 256, 256, 0, D), fob)]))
        return loads

    def mamba(layer):
        m = layer // 2
        MUL, ADD = ALU.mult, ALU.add
        with contextlib.ExitStack() as ph:
            sv = pt(ph, "m_sv", [128, 432]); sv_b = Buf()
            dma(sv[:], ssmvec_d[m][:, 0:432], [], [sv_b])
            dtb = sv[:, 0:64]; A_bc = sv[:, 64:128]; D_bc = sv[:, 128:192]
            act(A_bc, A_bc, AF.Exp, [sv_b], [sv_b])
            ts(A_bc, A_bc, -1.0, None, MUL, None, [sv_b], [sv_b])

            def CW(c, kk):
                return sv[:, 192 + c * 5 + kk:192 + c * 5 + kk + 1]
            ng = pt(ph, "m_ng", [128, 512]); ng_b = Buf()
            dt_tok = pt(ph, "m_dt", [128, 9, 64]); dt_b = Buf()
            a_tok = pt(ph, "m_a", [128, 9, 64]); a_b = Buf()
            tail = pt(ph, "m_tail", [128, 144]); tail_b = Buf()
            hist = pt(ph, "m_hist", [128, 144]); hist_b = Buf()
            tp, tpb = PS[7], PSb[7]
            for cg in range(24):
                si, wt, wb = ws.get(('mt', m, cg))
                w = wview(wt, KC, 256)
                for j in range(2):
                    c = cg * 2 + j
                    for kc in range(KC):
                        mm(tp[:, c * 3:c * 3 + 3], w[:, kc, j * 128:(j + 1) * 128], hT[:, kc, NTP - 3:NTP], kc == 0, kc == KC - 1, [wb, hT_b], [tpb])
                ws.release(si)
            act(tail[:], tp[:, 0:144], AF.Copy, [tpb], [tail_b])
            dma(o_ssm_conv_p[m], tail[:], [tail_b], [])
            pair_exchange(tail[:], 144, [tail_b], hist[:], hist_b)
            si, wt, wb = ws.get(('mdt', m))
            wdt = wview(wt, KC, 64)
            dtt = pt(ph, "m_dtt", [128, 64]); dtt_b = Buf()
            for ti in range(9):
                t0 = ti * 128; nt = 128 if ti < 8 else NTS
                ps, psb = PS[ti % 2], PSb[ti % 2]
                for kc in range(KC):
                    mm(ps[:nt, 0:64], hT[:, kc, t0:t0 + nt], wdt[:, kc, :], kc == 0, kc == KC - 1, [wb, hT_b], [psb])
                tt(dtt[:nt], ps[:nt, 0:64], dtb[:nt], ADD, [psb, sv_b], [dtt_b])
                act(dtt[:nt], dtt[:nt], AF.Exp, [dtt_b], [dtt_b])
                act(dt_tok[:nt, ti, :], dtt[:nt], AF.Ln, [dtt_b], [dt_b], bias=1.0)
                tt(a_tok[:nt, ti, :], dt_tok[:nt, ti, :], A_bc[:nt], MUL, [dt_b, sv_b], [a_b])
            ws.release(si)
            cd_all = pt(ph, "m_cdall", [128, 8, 64]); cd_b = Buf()
            eac_all = pt(ph, "m_eacall", [128, 8, 64]); eac_b = Buf()
            sdt_all = pt(ph, "m_sdtall", [128, 8, 64]); sdt_b = Buf()
            for ti in range(8):
                pq, pqb = PS[ti % 2], PSb[ti % 2]
                mm(pq[:, 0:64], ones_f[:], a_tok[:, ti, :], True, True, [a_b, ones_b], [pqb])
                mm(pq[:, 64:128], Umat, a_tok[:, ti, :], True, True, [a_b, consts_b], [pqb])
                act(eac_all[:, ti, :], pq[:, 64:128], AF.Copy, [pqb], [eac_b])
                tt(sdt_all[:, ti, :], pq[:, 0:64], eac_all[:, ti, :], ALU.subtract, [pqb, eac_b], [sdt_b])
                act(cd_all[:, ti, :], pq[:, 0:64], AF.Exp, [pqb], [cd_b])
            act(sdt_all[:], sdt_all[:], AF.Exp, [sdt_b], [sdt_b])
            tt(sdt_all[:], sdt_all[:], dt_tok[:, 0:8, :], MUL, [sdt_b, dt_b], [sdt_b])
            act(eac_all[:], eac_all[:], AF.Exp, [eac_b], [eac_b])
            cin = pt(ph, "m_cin", [128, NTP + 3]); cin_b = Buf()
            ctmp = pt(ph, "m_ctmp", [128, NT]); ctmp_b = Buf()
            xct = [pt(ph, f"m_xct{i}", [128, NT], BF16) for i in range(2)]; xct_b = [Buf(), Buf()]
            xbc = pt(ph, "m_xbc", [128, 2, NT], BF16); xbc_b = [Buf(), Buf()]
            x_tok = pt(ph, "m_xtok", [128, 9, 512], BF16); xtok_b = Buf()
            B_tok = pt(ph, "m_btok", [128, 9, 128], BF16); btok_b = Buf()
            C_s = pt(ph, "m_cs", [128, 128], BF16); cs_b = Buf()
            hs = pt(ph, "m_hs", [128, 6, 3, NTS]); hs_b = Buf()
            raws = pt(ph, "m_raws", [128, 6, NTS]); raws_b = Buf()
            hst = pt(ph, "m_h", [128, 512]); hst_b = Buf()
            hbf = pt(ph, "m_hbf", [128, 512], BF16); hbf_b = Buf()
            ynT = pt(ph, "m_ynT", [128, 4, 512], BF16); ynT_b = Buf()
            sm = pt(ph, "m_small", [128, 16]);
            eas = sm[:, 0:8]
            ss = sm[:, 8:9]
            eas_b, ss_b = Buf(), Buf()
            mcbT = pt(ph, "m_mcbT", [128, 128]); mcbT_b = Buf()
            decT = pt(ph, "m_decT", [128, 1024], BF16); decT_b = Buf()
            wT = pt(ph, "m_wT", [128, 1024], BF16); wT_b = Buf()
            negU = consts[:, 384:512]
            xs = pt(ph, "m_xs", [128, 512], BF16); xs_b = Buf()
            xdt = pt(ph, "m_xdt", [128, 512], BF16); xdt_b = Buf()
            f1 = pt(ph, "m_f1", [128, 512]); f1_b = Buf()
            f2 = pt(ph, "m_f2", [128, 512]); f2_b = Buf()
            zs = ctmp[:, 0:512]; zs_b = ctmp_b
            yn = pt(ph, "m_yn", [128, 512], BF16); yn_b = Buf()
            earep = pt(ph, "m_earep", [128, 128]); earep_b = Buf()
            dec = pt(ph, "m_dec", [128, NTS]); dec_b = Buf()
            xdtp, xdtp_b = zs, zs_b
            dtx = pt(ph, "m_dtx", [128, NTS * 4]); dtx_b = Buf()
            bcd = pt(ph, "m_bcd", [128, 512], BF16); bcd_b = Buf()
            st = ctmp[:, 0:1024].rearrange("p (b i n) -> p b i n", b=2, i=4); st_b = ctmp_b
            u2 = cin[:, 0:1024].rearrange("p (b i n) -> p b i n", b=2, i=4); u2_b = cin_b
            bc2, bc2_b = f1, f1_b
            ysp = pt(ph, "m_ysp", [128, NTS, 4]); ysp_b = Buf()
            pstb = PS[2][:].bitcast(BF16)
            pstv = pstb[:, 0:512].rearrange("p (q t) -> p q t", q=4)
            cnt = 0

            def v864(ap):
                return ap.rearrange("p (j q) -> p j q", q=64)

            def bc864(ap, nr=128):
                return ap.unsqueeze(2).to_broadcast([nr, 8, 64])

            for g in range(NG):
                hs8 = slice(8 * g, 8 * g + 8)
                dma(ng[:], ssmvec_d[m][:, 432 + g * 512:432 + (g + 1) * 512], [], [ng_b])
                dma(hs[:, 0:4], st_ssm_convT_d[m][:, 4 * g:4 * g + 4], [], [hs_b])
                dma(hs[:, 4], st_ssm_convT_d[m][:, 32 + g], [], [hs_b])
                dma(hs[:, 5], st_ssm_convT_d[m][:, 40 + g], [], [hs_b])
                si = None
                for ci in range(6):
                    if ci in (0, 2, 4):
                        if si is not None:
                            ws.release(si)
                        key = ('mx', m, g, ci // 2) if ci < 4 else ('mbc', m, g)
                        si, wt, wb = ws.get(key)
                        w = wview(wt, KC, 256)
                    jj = ci % 2
                    c = (4 * g + ci) if ci < 4 else (32 + g if ci == 4 else 40 + g)
                    for (t0, n) in TT:
                        ps, psb = PS[cnt % 2], PSb[cnt % 2]
                        cnt += 1
                        for kc in range(KC):
                            mm(ps[:, :n], w[:, kc, jj * 128:(jj + 1) * 128], hT[:, kc, t0:t0 + n], kc == 0, kc == KC - 1, [wb, hT_b], [psb])
                        if t0 < NTP:
                            act(cin[:, 3 + t0:3 + t0 + n], ps[:, :n], AF.Copy, [psb], [cin_b])
                        else:
                            act(raws[:, ci, :], ps[:, :n], AF.Copy, [psb], [raws_b])
                    cp(cin[:, 0:3], hist[:, c * 3:c * 3 + 3], [hist_b], [cin_b])
                    ts(ctmp[:, 0:NTP], cin[:, 0:NTP], CW(c, 0), CW(c, 4), MUL, ADD, [cin_b, sv_b], [ctmp_b])
                    for kk in range(1, 4):
                        stt(ctmp[:, 0:NTP], cin[:, kk:kk + NTP], CW(c, kk), ctmp[:, 0:NTP], MUL, ADD, [cin_b, sv_b, ctmp_b], [ctmp_b])
                    ts(ctmp[:, NTP:NT], hs[:, ci, 0, :], CW(c, 0), CW(c, 4), MUL, ADD, [hs_b, sv_b], [ctmp_b])
                    for kk in range(1, 3):
                        stt(ctmp[:, NTP:NT], hs[:, ci, kk, :], CW(c, kk), ctmp[:, NTP:NT], MUL, ADD, [hs_b, sv_b, ctmp_b], [ctmp_b])
                    stt(ctmp[:, NTP:NT], raws[:, ci, :], CW(c, 3), ctmp[:, NTP:NT], MUL, ADD, [raws_b, sv_b, ctmp_b], [ctmp_b])
                    if ci < 4:
                        src, srcb = xct[ci % 2][:], xct_b[ci % 2]
                    else:
                        src, srcb = xbc[:, ci - 4, :], xbc_b[ci - 4]
                    act(src, ctmp[:], AF.Silu, [ctmp_b], [srcb])
                    if ci < 5:
                        for half in range(2):
                            for q in range(4):
                                ti = half * 4 + q
                                tr(pstv[:, q, :], src[:, ti * 128:(ti + 1) * 128], ident_bf[:], [srcb, identbf_b], [PSb[2]])
                            if ci < 4:
                                cp(x_tok[:, half * 4:half * 4 + 4, ci * 128:(ci + 1) * 128], pstv, [PSb[2]], [xtok_b], eng='act' if half else 'dve')
                            else:
                                cp(B_tok[:, half * 4:half * 4 + 4, :], pstv, [PSb[2]], [btok_b], eng='act' if half else 'dve')
                    tr(pstb[:NTS, 0:128], src[:, NTP:NT], ident_bf[:], [srcb, identbf_b], [PSb[2]])
                    if ci < 4:
                        cp(x_tok[:NTS, 8, ci * 128:(ci + 1) * 128], pstb[:NTS, 0:128], [PSb[2]], [xtok_b])
                    elif ci == 4:
                        cp(B_tok[:NTS, 8, :], pstb[:NTS, 0:128], [PSb[2]], [btok_b])
                    else:
                        cp(C_s[:NTS, :], pstb[:NTS, 0:128], [PSb[2]], [cs_b])
                ws.release(si)
                for (lo, hi, cc0) in ((0, 4, 4 * g), (4, 5, 32 + g), (5, 6, 40 + g)):
                    dma(o_ssm_conv_sT[m][:, cc0:cc0 + hi - lo, 0:2, :], hs[:, lo:hi, 1:3, :], [hs_b], [])
                    dma(o_ssm_conv_sT[m][:, cc0:cc0 + hi - lo, 2, :], raws[:, lo:hi, :], [raws_b], [])
                wz = []; wzb = []; wo = []; wob = []; held = []
                for j in range(2):
                    s_, t_, b_ = ws.get(('mz', m, g, j)); held.append(s_); wz.append(wview(t_, KC, 256)); wzb.append(b_)
                for j in range(2):
                    s_, t_, b_ = ws.get(('mo', m, g, j)); held.append(s_); wo.append(wview(t_, 2, D)); wob.append(b_)

                def finish_tile(ti, nr):
                    t0 = ti * 128
                    zp, zpb = PS[0], PSb[0]
                    for j in range(2):
                        for kc in range(KC):
                            mm(zp[:nr, j * 256:(j + 1) * 256], hT[:, kc, t0:t0 + nr], wz[j][:, kc, :], kc == 0, kc == KC - 1, [wzb[j], hT_b], [zpb])
                    act(zs[:nr], zp[:nr], AF.Silu, [zpb], [zs_b])
                    tt(f2[:nr], f2[:nr], zs[:nr], MUL, [f2_b, zs_b], [f2_b])
                    act(f1[:nr], f2[:nr], AF.Square, [f2_b], [f1_b, ss_b], accum_out=ss[:nr])
                    act(ss[:nr], ss[:nr], AF.Sqrt, [ss_b], [ss_b], scale=1.0 / 512, bias=EPS)
                    k.op('dve', lambda e: e.reciprocal(out=ss[:nr], in_=ss[:nr]), reads=[ss_b], writes=[ss_b])
                    stt(yn[:nr], f2[:nr], ss[:nr], ng[:nr], MUL, MUL, [f2_b, ss_b, ng_b], [yn_b])
                    for rc in range(4):
                        tr(pstv[:, rc, 0:nr], yn[:nr, rc * 128:(rc + 1) * 128], ident_bf[:nr, :nr], [yn_b, identbf_b], [PSb[2]])
                    if ti < 8:
                        q = ti % 4
                        cp(ynT[:, :, q * 128:(q + 1) * 128], pstv, [PSb[2]], [ynT_b], eng='act')
                    else:
                        cp(ynT[:, :, 0:NTS], pstv[:, :, 0:NTS], [PSb[2]], [ynT_b], eng='act')
                    if ti in (3, 7, 8):
                        ncol = 512 if ti < 8 else NTS
                        x0 = (ti // 4) * 512 if ti < 8 else NTP
                        for dc in range(KC):
                            for rc in range(4):
                                mm(PS[7][:, :ncol], wo[rc // 2][:, rc % 2, dc * 128:(dc + 1) * 128], ynT[:, rc, 0:ncol], rc == 0, rc == 3, [wob[rc // 2], ynT_b], [PSb[7]])
                            tt(xT[:, dc, x0:x0 + ncol], xT[:, dc, x0:x0 + ncol], PS[7][:, :ncol], ADD, [PSb[7], xT_b[dc]], [xT_b[dc]])

                def chunk(ti, with_y):
                    a8 = a_tok[:, ti, hs8]; dt8 = dt_tok[:, ti, hs8]
                    t0 = ti * 128
                    pm, pmb = PS[4], PSb[4]
                    cd = cd_all[:, ti, hs8]; sdt = sdt_all[:, ti, hs8]; eac = eac_all[:, ti, hs8]
                    xv = v864(x_tok[:, ti, :])
                    if with_y:
                        BT = xbc[:, 0, t0:t0 + 128]; CT = xbc[:, 1, t0:t0 + 128]
                        mm(pm[:, 128:256], BT, CT, True, True, [xbc_b[0], xbc_b[1]], [pmb])
                        tt(mcbT[:], pm[:, 128:256], Umat, MUL, [pmb, consts_b], [mcbT_b])
                        tt(v864(xdt[:]), xv, bc864(dt8), MUL, [xtok_b, dt_b], [xdt_b])
                        yps, ypsb = PS[1], PSb[1]
                        arep3 = cin[:, 0:1024].rearrange("p (j q) -> p j q", j=8)
                        arepU3 = ctmp[:, 0:1024].rearrange("p (j q) -> p j q", j=8)
                        cp(arep3, a8.unsqueeze(2).to_broadcast([128, 8, 128]), [a_b], [cin_b])
                        tt(arepU3, arep3, negU.unsqueeze(1).to_broadcast([128, 8, 128]), MUL, [cin_b, consts_b], [ctmp_b])
                        for j in range(8):
                            pa = PS[5 + j // 4][:, (j % 4) * 128:(j % 4 + 1) * 128]; pab = PSb[5 + j // 4]
                            mm(pa, arep3[:, j, :], Umat, True, False, [cin_b, consts_b], [pab])
                            mm(pa, arepU3[:, j, :], ones_f[:], False, True, [ctmp_b, ones_b], [pab])
                        for hh in range(2):
                            act(decT[:, hh * 512:(hh + 1) * 512], PS[5 + hh][:], AF.Exp, [PSb[5 + hh]], [decT_b])
                        stt(wT[:].rearrange("p (j q) -> p j q", j=8), decT[:].rearrange("p (j q) -> p j q", j=8), 1.0,
                            mcbT[:].unsqueeze(1).to_broadcast([128, 8, 128]), ALU.min, MUL, [decT_b, mcbT_b], [wT_b])
                        for j in range(8):
                            mm(yps[:, j * 64:(j + 1) * 64], wT[:, j * 128:(j + 1) * 128], xdt[:, j * 64:(j + 1) * 64], True, True, [wT_b, xdt_b], [ypsb])
                        mm(PS[3][:], CT, hbf[:], True, True, [xbc_b[1], hbf_b], [PSb[3]])
                        tt(v864(f1[:]), v864(PS[3][:]), bc864(eac), MUL, [PSb[3], eac_b], [f1_b])
                        tt(f2[:], f1[:], yps[:], ADD, [f1_b, ypsb], [f2_b])
                        tt(v864(f1[:]), xv, bc864(D_bc[:, hs8]), MUL, [xtok_b, sv_b], [f1_b])
                        tt(f2[:], f2[:], f1[:], ADD, [f2_b, f1_b], [f2_b])
                        finish_tile(ti, 128)
                    tt(v864(xs[:]), xv, bc864(sdt), MUL, [xtok_b, sdt_b], [xs_b])
                    mm(PS[3][:], B_tok[:, ti, :], xs[:], True, True, [btok_b, xs_b], [PSb[3]])
                    tt(v864(f1[:]), v864(hst[:]), bc864(cd), MUL, [hst_b, cd_b], [f1_b])
                    tt(hst[:], f1[:], PS[3][:], ADD, [f1_b, PSb[3]], [hst_b])
                    if with_y:
                        act(hbf[:], hst[:], AF.Copy, [hst_b], [hbf_b])

                k.op('dve', lambda e: e.memset(hst[:], 0.0), writes=[hst_b])
                for ti in range(8):
                    chunk(ti, False)
                pair_exchange(hst[:], 512, [hst_b], hst[:], hst_b)
                act(hbf[:], hst[:], AF.Copy, [hst_b], [hbf_b])
                for ti in range(8):
                    chunk(ti, True)
                dma(o_ssm_p[m, g], hst[:], [hst_b], [])
                a_s = a_tok[:NTS, 8, hs8]; dt_s = dt_tok[:NTS, 8, hs8]
                act(eas[:NTS], a_s, AF.Exp, [a_b], [eas_b])
                cp(earep[:NTS].rearrange("p (j q) -> p j q", q=16), eas[:NTS].unsqueeze(2).to_broadcast([NTS, 8, 16]), [eas_b], [earep_b])
                mm(PS[5][:, 0:NTS], earep[:NTS, :], ident[0:NTS, 0:NTS], True, True, [earep_b, consts_b], [PSb[5]])
                cp(dec[:], PS[5][:, 0:NTS], [PSb[5]], [dec_b])
                tt(v864(f1[:NTS]), v864(x_tok[:NTS, 8, :]), bc864(dt_s, NTS), MUL, [xtok_b, dt_b], [f1_b])
                cp(xdtp[:NTS].rearrange("p (i q) -> p i q", i=4), f1[:NTS].rearrange("p (q i) -> p i q", i=4), [f1_b], [xdtp_b])
                for i in range(4):
                    mm(PS[6][:, i * NTS:(i + 1) * NTS], xdtp[:NTS, i * 128:(i + 1) * 128], ident[0:NTS, 0:NTS], True, True, [xdtp_b, consts_b], [PSb[6]])
                cp(dtx[:].rearrange("p (b i) -> p b i", i=4), PS[6][:, 0:4 * NTS].rearrange("p (i b) -> p b i", i=4), [PSb[6]], [dtx_b])
                for bb in range(NTS // 2):
                    b0 = bb * 2
                    srcv = st_ssm_d[m][b0:b0 + 2, hs8].rearrange("b j p n -> b (j p) n").rearrange("b (q i) n -> q b i n", i=4)
                    dstv = o_ssm_s[m][b0:b0 + 2, hs8].rearrange("b j p n -> b (j p) n").rearrange("b (q i) n -> q b i n", i=4)
                    dma(st, srcv, [], [st_b])
                    I2b = ident[0:NTS, b0:b0 + 2].unsqueeze(2).to_broadcast([NTS, 2, 128])
                    tt(bcd[:NTS, 0:256].rearrange("p (b n) -> p b n", n=128), B_tok[:NTS, 8, :].unsqueeze(1).to_broadcast([NTS, 2, 128]), I2b, MUL, [btok_b, consts_b], [bcd_b])
                    tt(bcd[:NTS, 256:512].rearrange("p (b n) -> p b n", n=128), C_s[:NTS, :].unsqueeze(1).to_broadcast([NTS, 2, 128]), I2b, MUL, [cs_b, consts_b], [bcd_b])
                    mm(PS[5][:, 0:512], ones_bf[0:NTS, :], bcd[:NTS, :], True, True, [bcd_b, ones_b], [PSb[5]])
                    act(bc2[:], PS[5][:], AF.Copy, [PSb[5]], [bc2_b])
                    tt(u2, dtx[:, b0 * 4:(b0 + 2) * 4].rearrange("p (b i) -> p b i", i=4).unsqueeze(3).to_broadcast([128, 2, 4, 128]),
                       bc2[:, 0:256].rearrange("p (b n) -> p b n", n=128).unsqueeze(2).to_broadcast([128, 2, 4, 128]), MUL, [dtx_b, bc2_b], [u2_b])
                    for bi in range(2):
                        stt(st[:, bi], st[:, bi], dec[:, b0 + bi:b0 + bi + 1], u2[:, bi], MUL, ADD, [st_b, dec_b, u2_b], [st_b])
                    dma(dstv, st, [st_b], [])
                    tt(u2, st, bc2[:, 256:512].rearrange("p (b n) -> p b n", n=128).unsqueeze(2).to_broadcast([128, 2, 4, 128]), MUL, [st_b, bc2_b], [u2_b])
                    k.op('dve', lambda e: e.tensor_reduce(out=ysp[:, b0:b0 + 2, :], in_=u2, axis=AX.X, op=ADD), reads=[u2_b], writes=[ysp_b])
                for i in range(4):
                    tr(PS[6][:NTS, i * 128:(i + 1) * 128], ysp[:, :, i], ident, [ysp_b, consts_b], [PSb[6]])
                cp(f2[:NTS].rearrange("p (q i) -> p i q", i=4), PS[6][:NTS, 0:512].rearrange("p (i q) -> p i q", i=4), [PSb[6]], [f2_b])
                tt(v864(f1[:NTS]), v864(x_tok[:NTS, 8, :]), bc864(D_bc[:NTS, hs8], NTS), MUL, [xtok_b, sv_b], [f1_b])
                tt(f2[:NTS], f2[:NTS], f1[:NTS], ADD, [f2_b, f1_b], [f2_b])
                finish_tile(8, NTS)
                for s_ in held:
                    ws.release(s_)
            k.barrier()


    def layer_loads(layer):
        loads = []
        if CFG['ffn']:
            loads += ffn_loads(layer, 0)
        if CFG['mixer']:
            loads += (mamba_loads(layer) if layer % 2 == 0 else lru_loads(layer))
        if CFG['ffn']:
            loads += ffn_loads(layer, 1)
        if CFG['ple']:
            loads += ple_loads(layer)
        return loads

    gather_layer(0)
    ws.plan_loads(layer_loads(0))
    for layer in range(L):
        gather_layer(layer + 1)
        if layer + 1 < L:
            ws.plan_loads(layer_loads(layer + 1))
        if CFG['ffn']:
            rmsnorm(0, layer, hT, hT_b)
            ffn(layer, 0)
        if CFG['mixer']:
            rmsnorm(1, layer, hT, hT_b)
            if layer % 2 == 0:
                mamba(layer)
            else:
                lru(layer)
        if CFG['ffn']:
            rmsnorm(2, layer, hT, hT_b)
            ffn(layer, 1)
        if CFG['ple']:
            rmsnorm(3, layer, hT, hT_b)
            ple(layer)

    rmsnorm(4, 0, xT, xT_b)
    for c in range(KC):
        dma(yT_d[c * 128:(c + 1) * 128, :], xT[:, c, :], [xT_b[c]], [])
    k.finish()
    es.close()
    return nc, k


def build_program():
    nc1, k1 = build(None)
    plan = k1.resolve()
    nc2, k2 = build(plan)
    for e in ENGS:
        assert k1.nops[e] == k2.nops[e]
    return nc2


def _consts():
    c = np.zeros((128, 512), np.float32)
    c[:, 0:128] = np.eye(128, dtype=np.float32)
    c[:, 128:256] = np.triu(np.ones((128, 128), np.float32))
    c[:, 384:512] = -np.triu(np.ones((128, 128), np.float32))
    return c


def _fm(v):
    v = np.asarray(v, np.float32)
    return np.ascontiguousarray(v.reshape(-1, 128).T)


def kernel(**inp):
    f32 = np.float32
    nc = build_program()
    A = {n: np.asarray(v) for n, v in inp.items()}
    x_prompt = A['x_prompt']; x_sample = A['x_sample']; p_prompt = A['p_prompt']; p_sample = A['p_sample']
    gl = [A['norm_ffn_pre'], A['norm_mix'], A['norm_ffn_post'], A['norm_ple']]
    gains = np.zeros((128, (4 * DEPTH + 1) * KC), f32)
    for kind in range(4):
        for l in range(DEPTH):
            j = (kind * DEPTH + l) * KC
            gains[:, j:j + KC] = _fm(gl[kind][l])
    gains[:, 4 * DEPTH * KC:] = _fm(A['norm_final'])
    ssmvec = np.zeros((2, 128, NSV), f32)
    lruvec = np.zeros((2, 128, NLV * KC), f32)
    for m in range(2):
        ssmvec[m, :, 0:64] = A['ssm_dt_bias'][m][None, :]
        ssmvec[m, :, 64:128] = A['ssm_a_log'][m][None, :]
        ssmvec[m, :, 128:192] = A['ssm_d'][m][None, :]
        cw = np.concatenate([A['ssm_conv_w'][m], A['ssm_conv_b'][m][None, :]], 0)
        ssmvec[m, :, 192:192 + 240] = cw.reshape(5, 48, 128).transpose(2, 1, 0).reshape(128, 240)
        ssmvec[m, :, 432:] = A['ssm_norm'][m][None, :]
        vs = [A['lru_b_x'][m], A['lru_b_y'][m], A['lru_conv_w'][m][0], A['lru_conv_w'][m][1], A['lru_conv_w'][m][2], A['lru_conv_w'][m][3],
              A['lru_conv_b'][m], A['lru_b_rgate'][m], A['lru_b_igate'][m], A['lru_a_param'][m], A['lru_b_out'][m]]
        for i, v in enumerate(vs):
            lruvec[m, :, i * KC:(i + 1) * KC] = _fm(v)
    in_maps = []
    for c in range(8):
        s, hb = c // 2, c % 2
        xt = np.concatenate([x_prompt[s, hb * NTP:(hb + 1) * NTP, :], x_sample[c * NTS:(c + 1) * NTS, 0, :]], 0)
        pt_ = np.concatenate([p_prompt[:, s, hb * NTP:(hb + 1) * NTP, :], p_sample[:, c * NTS:(c + 1) * NTS, 0, :]], 1)
        cs = _consts()
        cs[:, 256] = float(hb)
        bs = slice(c * NTS, (c + 1) * NTS)
        m = {"xT": np.ascontiguousarray(xt.T), "pT": np.ascontiguousarray(pt_.transpose(0, 2, 1)), "consts": cs, "gains": gains,
             "ssmvec": ssmvec, "lruvec": lruvec,
             "st_ssm": np.ascontiguousarray(A['state_ssm'][:, bs]),
             "st_ssm_convT": np.ascontiguousarray(A['state_ssm_conv'][:, bs].reshape(2, NTS, 3, 48, 128).transpose(0, 4, 3, 2, 1)),
             "st_lruT": np.ascontiguousarray(A['state_lru'][:, bs].reshape(2, NTS, KC, 128).transpose(0, 3, 2, 1)),
             "st_lru_convT": np.ascontiguousarray(A['state_lru_conv'][:, bs].reshape(2, NTS, 3, KC, 128).transpose(0, 4, 3, 2, 1))}
        for n, (nl, rows, cols) in WSPEC.items():
            m[n + "_full"] = A[n].reshape(nl, rows, cols)
        in_maps.append(m)
    res = run_bass_kernel_spmd(nc, in_maps, core_ids=list(range(8)))
    R = [{n: np.asarray(v) for n, v in r.items()} for r in res.results]
    y_prompt = np.zeros((4, 2048, D), f32); y_sample = np.zeros((128, 1, D), f32)
    ssm_p = np.zeros((2, 4, NH, 64, NST), f32); ssm_conv_p = np.zeros((2, 4, 3, CONVD), f32)
    lru_p = np.zeros((2, 4, D), f32); lru_conv_p = np.zeros((2, 4, 3, D), f32)
    ssm_s = np.zeros((2, 128, NH, 64, NST), f32); ssm_conv_s = np.zeros((2, 128, 3, CONVD), f32)
    lru_s = np.zeros((2, 128, D), f32); lru_conv_s = np.zeros((2, 128, 3, D), f32)
    for c in range(8):
        s, hb = c // 2, c % 2
        r = R[c]
        bs = slice(c * NTS, (c + 1) * NTS)
        yt = r["yT"].T
        y_prompt[s, hb * NTP:(hb + 1) * NTP] = yt[:NTP]
        y_sample[bs, 0] = yt[NTP:]
        ssm_s[:, bs] = r["o_ssm_s"]
        ssm_conv_s[:, bs] = r["o_ssm_conv_sT"].transpose(0, 4, 3, 2, 1).reshape(2, NTS, 3, CONVD)
        lru_s[:, bs] = r["o_lru_sT"].transpose(0, 3, 2, 1).reshape(2, NTS, D)
        lru_conv_s[:, bs] = r["o_lru_conv_sT"].transpose(0, 4, 3, 2, 1).reshape(2, NTS, 3, D)
        if hb == 1:
            ssm_p[:, s] = r["o_ssm_p"].reshape(2, NG, NST, 8, 64).transpose(0, 1, 3, 4, 2).reshape(2, NH, 64, NST)
            ssm_conv_p[:, s] = r["o_ssm_conv_p"].reshape(2, 128, 48, 3).transpose(0, 3, 2, 1).reshape(2, 3, CONVD)
            lru_p[:, s] = r["o_lru_p"].transpose(0, 2, 1).reshape(2, D)
            lru_conv_p[:, s] = r["o_lru_conv_p"].reshape(2, 128, KC, 3).transpose(0, 3, 2, 1).reshape(2, 3, D)
    return (y_prompt, y_sample, ssm_p, ssm_conv_p, lru_p, lru_conv_p, ssm_s, ssm_conv_s, lru_s, lru_conv_s)
```

```python
import os
import contextlib
import numpy as np
import concourse.bass as bass
import concourse.mybir as mybir
from concourse.bass_utils import run_bass_kernel_spmd

F32 = mybir.dt.float32
BF16 = mybir.dt.bfloat16
AF = mybir.ActivationFunctionType
ALU = mybir.AluOpType
AX = mybir.AxisListType

D = 2048; FF = 5632; DEPTH = 4; PLE = 256
NTP = 1024; NTS = 16; NT = NTP + NTS
KC = D // 128
DI = 4096; NG = 8; NST = 128; CONVD = 6144; INDIM = 10304; NH = 64
EPS = 1e-6
TT = [(0, 512), (512, 512), (1024, 16)]
ENGS = ['pe', 'act', 'dve', 'pool', 'sp']
NPOOL = 12
SEMCH = 30000

CFG = {
    'layers': int(os.environ.get('MK_LAYERS', '4')),
    'ffn': int(os.environ.get('MK_FFN', '1')),
    'mixer': int(os.environ.get('MK_MIXER', '1')),
    'ple': int(os.environ.get('MK_PLE', '1')),
}


class Buf:
    __slots__ = ('w', 'r', 'name')

    def __init__(self, name=''):
        self.w = None
        self.r = {}
        self.name = name


class KB:
    def __init__(self, nc, plan=None):
        self.nc = nc
        self.dry = plan is None
        self.plan = plan
        self.ops = {e: [] for e in ENGS}
        self.nops = {e: 0 for e in ENGS}
        self.ndma = {e: 0 for e in ENGS + ['cc']}
        self.lastd = {}
        self.lastreal = {}
        self.eng = {'pe': nc.tensor, 'act': nc.scalar, 'dve': nc.vector, 'pool': nc.gpsimd, 'sp': nc.sync}
        if not self.dry:
            self.csem = {e: [nc.alloc_semaphore(f"c_{e}_{i}") for i in range(plan['ncsem'][e])] for e in ENGS}
            self.dsem = {e: [nc.alloc_semaphore(f"d_{e}_{i}") for i in range(NPOOL)] for e in ('sp', 'pool', 'cc')}

    def op(self, eng, fn, reads=(), writes=(), dma=False, extra=()):
        i = self.nops[eng]
        self.nops[eng] = i + 1
        if fn is not None and not dma:
            self.lastreal[eng] = i
        if dma:
            q = 'cc' if dma == 'cc' else eng
            inc = 1 if dma == 'cc' else 16
            n = self.ndma[q]
            self.ndma[q] = n + 1
            k = n % NPOOL
            val = inc * (n // NPOOL + 1)
            tok = ('d', q, k, val)
        else:
            tok = ('c', eng, i)
        if self.dry:
            deps = set(extra)
            for b in reads:
                if b.w is not None:
                    deps.add(b.w)
            for b in writes:
                if b.w is not None:
                    deps.add(b.w)
                deps.update(b.r.values())
            if dma and val > inc:
                deps.add(('d', q, k, val - inc))
            self.ops[eng].append(deps)
            for b in reads:
                key = (tok[0], tok[1]) if tok[0] == 'c' else tok
                b.r[key] = tok
            for b in writes:
                b.w = tok
                b.r = {}
            if dma:
                self.lastd[(q, k)] = tok
            return tok
        waits, sig = self.plan['ops'][eng][i]
        e = self.eng[eng]
        for (kind, a, b, v) in waits:
            if kind == 'c':
                e.wait_ge(self.csem[a][b], v)
            else:
                e.wait_ge(self.dsem[a][b], v)
        if fn is not None:
            inst = fn(e)
            if dma == 'cc':
                inst.then_inc(self.dsem['cc'][k])
            elif dma:
                inst.then_inc(self.dsem[eng][k], 16)
            elif sig is not None:
                inst.then_inc(self.csem[eng][sig], 1)
        else:
            assert sig is None
        return tok

    def last_tokens(self):
        toks = []
        for e in ENGS:
            if e in self.lastreal:
                toks.append(('c', e, self.lastreal[e]))
        toks.extend(self.lastd.values())
        return toks

    def barrier(self):
        if self.dry:
            toks = self.last_tokens()
        else:
            toks = ()
        for e in ENGS:
            self.op(e, None, extra=[t for t in toks if not (t[0] == 'c' and t[1] == e)])

    def finish(self):
        toks = list(self.lastd.values()) if self.dry else ()
        self.op('sp', None, extra=toks)

    def resolve(self):
        need = {e: set() for e in ENGS}
        waits_all = {e: [] for e in ENGS}
        for e in ENGS:
            seen = {}
            for i, deps in enumerate(self.ops[e]):
                ws = []
                for t in sorted(deps, key=lambda t: (t[0], t[1], t[2], t[3] if len(t) > 3 else 0)):
                    if t[0] == 'c':
                        f, j = t[1], t[2]
                        if f == e and (e in ('pe', 'sp') or j >= i):
                            continue
                        if seen.get(('c', f), -1) >= j:
                            continue
                        seen[('c', f)] = j
                        need[f].add(j)
                        ws.append(t)
                    else:
                        key = ('d', t[1], t[2])
                        if seen.get(key, 0) >= t[3]:
                            continue
                        seen[key] = t[3]
                        ws.append(t)
                waits_all[e].append(ws)
        signo = {}
        ncsem = {}
        for e in ENGS:
            n = 0
            for j in sorted(need[e]):
                n += 1
                signo[(e, j)] = n
            ncsem[e] = max(1, (n + SEMCH - 1) // SEMCH)
        plan_ops = {}
        for e in ENGS:
            lst = []
            for i, ws in enumerate(waits_all[e]):
                out = []
                for t in ws:
                    if t[0] == 'c':
                        n = signo[(t[1], t[2])]
                        out.append(('c', t[1], (n - 1) // SEMCH, (n - 1) % SEMCH + 1))
                    else:
                        out.append(('d', t[1], t[2], t[3]))
                sig = None
                if (e, i) in signo:
                    sig = (signo[(e, i)] - 1) // SEMCH
                lst.append((out, sig))
            plan_ops[e] = lst
        return {'ops': plan_ops, 'ncsem': ncsem}


class WStream:
    def __init__(self, k, slots, sbufs):
        self.k = k
        self.slots = slots
        self.sbufs = sbufs
        self.free = list(range(len(slots)))
        self.pending = []
        self.issued = []
        self.pi = 0

    def plan_loads(self, loads):
        self.pending.extend(loads)
        self.pump()

    def pump(self):
        while self.free and self.pi < len(self.pending):
            key, parts = self.pending[self.pi]
            self.pi += 1
            s = self.free.pop(0)
            tile = self.slots[s]
            for (dst_fn, src, srcb) in parts:
                dst = dst_fn(tile)
                self.k.op('pool', lambda e, dst=dst, src=src: e.dma_start(out=dst, in_=src), reads=[srcb], writes=[self.sbufs[s]], dma=True)
            self.issued.append((key, s))

    def get(self, key):
        assert self.issued, f"no issued load for {key}"
        k0, s = self.issued.pop(0)
        assert k0 == key, (k0, key)
        return s, self.slots[s], self.sbufs[s]

    def release(self, s):
        self.free.append(s)
        self.pump()


WSPEC = {
    'ffn_w_gate': (4, 2 * D, FF), 'ffn_w_up': (4, 2 * D, FF), 'ffn_w_down': (4, 2 * FF, D),
    'ple_w_gate': (4, D, D), 'ple_w_proj': (4, PLE, D),
    'ssm_w_in': (2, D, INDIM), 'ssm_w_out': (2, DI, D),
    'lru_w_x': (2, D, D), 'lru_w_y': (2, D, D), 'lru_w_out': (2, D, D),
    'lru_w_rgate': (2, D, 256), 'lru_w_igate': (2, D, 256),
}
NSV = 64 * 3 + 48 * 5 + DI
NLV = 11


def build(plan=None):
    nc = bass.Bass("TRN2", target_bir_lowering=False)
    k = KB(nc, plan)
    L = CFG['layers']

    def din(name, shape):
        return nc.dram_tensor(name, list(shape), F32, kind="ExternalInput").ap()

    def dout(name, shape):
        return nc.dram_tensor(name, list(shape), F32, kind="ExternalOutput").ap()

    xT_d = din("xT", [D, NT])
    pT_d = din("pT", [DEPTH, PLE, NT])
    consts_d = din("consts", [128, 4 * 128])
    gains_d = din("gains", [128, (4 * DEPTH + 1) * KC])
    wsh_d = {n: din(n + "_full", [nl, rows, cols]) for n, (nl, rows, cols) in WSPEC.items()}
    ssmvec_d = din("ssmvec", [2, 128, NSV])
    lruvec_d = din("lruvec", [2, 128, NLV * KC])
    st_ssm_d = din("st_ssm", [2, NTS, NH, 64, NST])
    st_ssm_convT_d = din("st_ssm_convT", [2, 128, 48, 3, NTS])
    st_lruT_d = din("st_lruT", [2, 128, KC, NTS])
    st_lru_convT_d = din("st_lru_convT", [2, 128, KC, 3, NTS])
    yT_d = dout("yT", [D, NT])
    o_ssm_p = dout("o_ssm_p", [2, NG, 128, 512])
    o_ssm_conv_p = dout("o_ssm_conv_p", [2, 128, 144])
    o_lru_p = dout("o_lru_p", [2, 128, KC])
    o_lru_conv_p = dout("o_lru_conv_p", [2, 128, KC * 3])
    o_ssm_s = dout("o_ssm_s", [2, NTS, NH, 64, NST])
    o_ssm_conv_sT = dout("o_ssm_conv_sT", [2, 128, 48, 3, NTS])
    o_lru_sT = dout("o_lru_sT", [2, 128, KC, NTS])
    o_lru_conv_sT = dout("o_lru_conv_sT", [2, 128, KC, 3, NTS])

    es = contextlib.ExitStack()
    uid = [0]

    def pt(ph, name, shape, dt=F32):
        uid[0] += 1
        return ph.enter_context(nc.sbuf_tensor(f"{name}_{uid[0]}", list(shape), dt))

    def sb(name, shape, dt=F32):
        return es.enter_context(nc.sbuf_tensor(name, list(shape), dt))

    def mm(ps, lhsT, rhs, st, sp, R, W):
        k.op('pe', lambda e: e.matmul(ps, lhsT=lhsT, rhs=rhs, start=st, stop=sp), reads=R, writes=W)

    def tr(ps, in_, idn, R, W):
        k.op('pe', lambda e: e.transpose(ps, in_, idn), reads=R, writes=W)

    def act(out, in_, func, R, W, **kw):
        k.op('act', lambda e: e.activation(out=out, in_=in_, func=func, **kw), reads=R, writes=W)

    def tt(out, in0, in1, op, R, W, eng='dve'):
        k.op(eng, lambda e: e.tensor_tensor(out=out, in0=in0, in1=in1, op=op), reads=R, writes=W)

    def ts(out, in0, s1, s2, op0, op1, R, W):
        if s2 is None:
            k.op('dve', lambda e: e.tensor_scalar(out=out, in0=in0, scalar1=s1, scalar2=None, op0=op0), reads=R, writes=W)
        else:
            k.op('dve', lambda e: e.tensor_scalar(out=out, in0=in0, scalar1=s1, scalar2=s2, op0=op0, op1=op1), reads=R, writes=W)

    def stt(out, in0, scalar, in1, op0, op1, R, W):
        k.op('dve', lambda e: e.scalar_tensor_tensor(out=out, in0=in0, scalar=scalar, in1=in1, op0=op0, op1=op1), reads=R, writes=W)

    def cp(out, in_, R, W, eng='dve'):
        if eng == 'act':
            k.op('act', lambda e: e.activation(out=out, in_=in_, func=AF.Copy), reads=R, writes=W)
        else:
            k.op(eng, lambda e: e.tensor_copy(out=out, in_=in_), reads=R, writes=W)

    def dma(out, in_, R, W, q='sp'):
        k.op(q, lambda e: e.dma_start(out=out, in_=in_), reads=R, writes=W, dma=True)

    xT = sb("xT_sb", [128, KC, NT]); xT_b = [Buf(f"x{c}") for c in range(KC)]
    hT = sb("hT_sb", [128, KC, NT], BF16); hT_b = Buf("hT")
    consts = sb("consts_sb", [128, 512]); consts_b = Buf("consts")
    gains = sb("gains_sb", [128, (4 * DEPTH + 1) * KC]); gains_b = Buf("gains")
    ones_bf = sb("ones_bf", [128, 128], BF16); ones_b = Buf("ones")
    ones_f = sb("ones_f", [128, 128], F32)
    ident_bf = sb("ident_bf", [128, 128], BF16); identbf_b = Buf("identbf")
    NS = 5
    wslots = [sb(f"wslot{i}", [128, 4096], BF16) for i in range(NS)]
    wbufs = [Buf(f"w{i}") for i in range(NS)]
    ws = WStream(k, wslots, wbufs)
    PS = [es.enter_context(nc.psum_tensor(f"ps{i}", [128, 512], F32)) for i in range(8)]
    PSb = [Buf(f"ps{i}") for i in range(8)]
    ident = consts[:, 0:128]
    Umat = consts[:, 128:256]
    isB = consts[:, 256:257]

    for c in range(KC):
        dma(xT[:, c, :], xT_d[c * 128:(c + 1) * 128, :], [], [xT_b[c]])
    dma(consts[:], consts_d, [], [consts_b])
    dma(gains[:], gains_d, [], [gains_b])
    k.op('dve', lambda e: e.memset(ones_bf[:], 1.0), writes=[ones_b])
    k.op('dve', lambda e: e.memset(ones_f[:], 1.0), writes=[ones_b])
    cp(ident_bf[:], ident, [consts_b], [identbf_b])

    wf = {n: {} for n in WSPEC}

    def gather(name, l):
        wf[name][l] = (wsh_d[name][l], Buf())

    def gather_layer(layer):
        if layer >= L:
            return
        if CFG['ffn']:
            for n in ('ffn_w_gate', 'ffn_w_up', 'ffn_w_down'):
                gather(n, layer)
        if CFG['mixer']:
            m = layer // 2
            if layer % 2 == 0:
                for n in ('ssm_w_in', 'ssm_w_out'):
                    gather(n, m)
            else:
                for n in ('lru_w_x', 'lru_w_rgate', 'lru_w_igate', 'lru_w_y', 'lru_w_out'):
                    gather(n, m)
        if CFG['ple']:
            for n in ('ple_w_gate', 'ple_w_proj'):
                gather(n, layer)

    xcount = [0]

    def pair_exchange(src, F, R, dst, dst_b):
        xcount[0] += 1
        ci = nc.dram_tensor(f"xch_i{xcount[0]}", [128, F], F32)
        co = nc.dram_tensor(f"xch_o{xcount[0]}", [256, F], F32)
        cib, cob = Buf(), Buf()
        dma(ci.ap(), src, R, [cib])
        k.op('pool', lambda e: e.collective_compute("AllGather", ALU.bypass, replica_groups=[[0, 1], [2, 3], [4, 5], [6, 7]],
                                                    ins=[ci.ap().opt()], outs=[co.ap().opt()]), reads=[cib], writes=[cob], dma='cc')
        dma(dst, co.ap()[0:128, :], [cob], [dst_b])
        ts(dst, dst, isB, None, ALU.mult, None, [dst_b, consts_b], [dst_b])

    def gain_col(kind, layer, c):
        j = (kind * DEPTH + layer) * KC + c
        return gains[:, j:j + 1]

    def rmsnorm(kind, layer, out_tile, out_buf):
        with contextlib.ExitStack() as ph:
            sq = [pt(ph, f"nsq{i}", [128, 512], BF16) for i in range(4)]
            sq_b = [Buf() for _ in range(4)]
            rstd = pt(ph, "nrstd", [128, 512], F32); rstd_b = Buf()
            for ti, (t0, n) in enumerate(TT):
                ps, psb = PS[6 + (ti % 2)], PSb[6 + (ti % 2)]
                for c in range(KC):
                    s, sbf = sq[c % 4], sq_b[c % 4]
                    act(s[:, :n], xT[:, c, t0:t0 + n], AF.Square, [xT_b[c]], [sbf])
                    mm(ps[:, :n], ones_bf[:], s[:, :n], c == 0, c == KC - 1, [sbf, ones_b], [psb])
                act(rstd[:, :n], ps[:, :n], AF.Sqrt, [psb], [rstd_b], scale=1.0 / D, bias=EPS)
                k.op('dve', lambda e: e.reciprocal(out=rstd[:, :n], in_=rstd[:, :n]), reads=[rstd_b], writes=[rstd_b])
                for c in range(KC):
                    wr = [out_buf] if not isinstance(out_buf, list) else [out_buf[c]]
                    stt(out_tile[:, c, t0:t0 + n], xT[:, c, t0:t0 + n], gain_col(kind, layer, c), rstd[:, :n], ALU.mult, ALU.mult,
                        [xT_b[c], rstd_b, gains_b], wr)
            k.barrier()

    def wview(t, kc, f):
        return t[:, 0:kc * f].rearrange("p (kc f) -> p kc f", kc=kc)

    def wsrc(full, r0, nr, c0, ncol):
        return full[r0:r0 + nr, c0:c0 + ncol].rearrange("(kc p) f -> p kc f", p=128)

    G = 2
    NGRP = FF // (128 * G)

    def ffn_loads(layer, half):
        loads = []
        GD = 2
        pend = []
        for g in range(NGRP):
            f0 = g * 128 * G
            for nm, wn in (('g', 'ffn_w_gate'), ('u', 'ffn_w_up')):
                full, fb = wf[wn][layer]
                loads.append(((nm, layer, half, g), [(lambda t: wview(t, KC, 128 * G), wsrc(full, half * D, D, f0, 128 * G), fb)]))
            full, fb = wf['ffn_w_down'][layer]
            pend.append((('d', layer, half, g), [(lambda t: wview(t, G, D), wsrc(full, half * FF + f0, 128 * G, 0, D), fb)]))
            if g % GD == GD - 1:
                loads += pend
                pend = []
        return loads

    def ffn(layer, half):
        GD = 2
        with contextlib.ExitStack() as ph:
            actt = pt(ph, "ffn_act", [128, G * GD, NT], BF16); act_b = [Buf() for _ in range(G * GD)]
            sg = [pt(ph, f"ffn_sg{i}", [128, 512], F32) for i in range(2)]
            sg_b = [Buf(), Buf()]
            cnt = 0
            cntd = 0
            for g in range(NGRP):
                sub = g % GD
                sgi, wg_t, wg_b = ws.get(('g', layer, half, g))
                sui, wu_t, wu_b = ws.get(('u', layer, half, g))
                wg = wview(wg_t, KC, 128 * G); wu = wview(wu_t, KC, 128 * G)
                for fc in range(G):
                    af = sub * G + fc
                    for (t0, n) in TT:
                        pg, pgb = PS[cnt % 2], PSb[cnt % 2]
                        pu, pub = PS[2 + cnt % 2], PSb[2 + cnt % 2]
                        s, s_b = sg[cnt % 2], sg_b[cnt % 2]
                        cnt += 1
                        for kc in range(KC):
                            mm(pg[:, :n], wg[:, kc, fc * 128:(fc + 1) * 128], hT[:, kc, t0:t0 + n], kc == 0, kc == KC - 1, [wg_b, hT_b], [pgb])
                        for kc in range(KC):
                            mm(pu[:, :n], wu[:, kc, fc * 128:(fc + 1) * 128], hT[:, kc, t0:t0 + n], kc == 0, kc == KC - 1, [wu_b, hT_b], [pub])
                        act(s[:, :n], pg[:, :n], AF.Silu, [pgb], [s_b])
                        tt(actt[:, af, t0:t0 + n], s[:, :n], pu[:, :n], ALU.mult, [s_b, pub], [act_b[af]])
                ws.release(sgi); ws.release(sui)
                if sub != GD - 1:
                    continue
                wds = []
                for gg in range(g - GD + 1, g + 1):
                    sdi, wd_t, wd_b = ws.get(('d', layer, half, gg))
                    wds.append((sdi, wview(wd_t, G, D), wd_b))
                nf = G * GD
                for dc in range(KC):
                    for (t0, n) in TT:
                        pd, pdb = PS[4 + cntd % 2], PSb[4 + cntd % 2]
                        cntd += 1
                        for af in range(nf):
                            _, wdn, wd_b = wds[af // G]
                            mm(pd[:, :n], wdn[:, af % G, dc * 128:(dc + 1) * 128], actt[:, af, t0:t0 + n], af == 0, af == nf - 1, [wd_b, act_b[af]], [pdb])
                        stt(xT[:, dc, t0:t0 + n], pd[:, :n], 0.5, xT[:, dc, t0:t0 + n], ALU.mult, ALU.add, [pdb, xT_b[dc]], [xT_b[dc]])
                for (sdi, _, _) in wds:
                    ws.release(sdi)
            k.barrier()

    def ple_loads(layer):
        loads = []
        full, fb = wf['ple_w_gate'][layer]
        for dg in range(D // 256):
            loads.append((('pg', layer, dg), [(lambda t: wview(t, KC, 256), wsrc(full, 0, D, dg * 256, 256), fb)]))
        full, fb = wf['ple_w_proj'][layer]
        loads.append((('pp', layer), [(lambda t: wview(t, 2, D), wsrc(full, 0, PLE, 0, D), fb)]))
        return loads

    def ple(layer):
        with contextlib.ExitStack() as ph:
            pf = pt(ph, "ple_pf", [128, 2, NT], F32); pf_b = Buf()
            pb = pt(ph, "ple_pb", [128, 2, NT], BF16); pb_b = Buf()
            gs = pt(ph, "ple_gs", [128, KC, NT], BF16); gs_b = [Buf() for _ in range(KC)]
            tmp = [pt(ph, f"ple_tmp{i}", [128, 512], F32) for i in range(2)]
            tmp_b = [Buf(), Buf()]
            for kc in range(2):
                dma(pf[:, kc, :], pT_d[layer, kc * 128:(kc + 1) * 128, :], [], [pf_b])
            act(pb[:], pf[:], AF.Copy, [pf_b], [pb_b])
            cnt = 0
            for dg in range(D // 256):
                si, wt, wb = ws.get(('pg', layer, dg))
                wg = wview(wt, KC, 256)
                for j in range(2):
                    dc = dg * 2 + j
                    for (t0, n) in TT:
                        pg, pgb = PS[cnt % 2], PSb[cnt % 2]
                        cnt += 1
                        for kc in range(KC):
                            mm(pg[:, :n], wg[:, kc, j * 128:(j + 1) * 128], hT[:, kc, t0:t0 + n], kc == 0, kc == KC - 1, [wb, hT_b], [pgb])
                        act(gs[:, dc, t0:t0 + n], pg[:, :n], AF.Sigmoid, [pgb], [gs_b[dc]])
                ws.release(si)
            si, wt, wb = ws.get(('pp', layer))
            wp = wview(wt, 2, D)
            for dc in range(KC):
                for (t0, n) in TT:
                    pp, ppb = PS[2 + cnt % 2], PSb[2 + cnt % 2]
                    t, t_b = tmp[cnt % 2], tmp_b[cnt % 2]
                    cnt += 1
                    for kc in range(2):
                        mm(pp[:, :n], wp[:, kc, dc * 128:(dc + 1) * 128], pb[:, kc, t0:t0 + n], kc == 0, kc == 1, [wb, pb_b], [ppb])
                    tt(t[:, :n], gs[:, dc, t0:t0 + n], pp[:, :n], ALU.mult, [gs_b[dc], ppb], [t_b])
                    tt(xT[:, dc, t0:t0 + n], xT[:, dc, t0:t0 + n], t[:, :n], ALU.add, [t_b, xT_b[dc]], [xT_b[dc]])
            ws.release(si)
            k.barrier()

    def lru_loads(layer):
        m = layer // 2
        loads = []
        fx, fxb = wf['lru_w_x'][m]; fy, fyb = wf['lru_w_y'][m]; fo, fob = wf['lru_w_out'][m]
        fr, frb = wf['lru_w_rgate'][m]; fi, fib = wf['lru_w_igate'][m]
        for nb in range(8):
            loads.append((('lt', m, nb), [(lambda t: wview(t, KC, 256), wsrc(fx, 0, D, nb * 256, 256), fxb)]))
        for nb in range(8):
            loads.append((('lx', m, nb), [(lambda t: wview(t, KC, 256), wsrc(fx, 0, D, nb * 256, 256), fxb)]))
            loads.append((('lg', m, nb), [(lambda t: wview(t, 4, 256)[:, 0:2, :], wsrc(fr, nb * 256, 256, 0, 256), frb),
                                         (lambda t: wview(t, 4, 256)[:, 2:4, :], wsrc(fi, nb * 256, 256, 0, 256), fib)]))
            loads.append((('ly', m, nb), [(lambda t: wview(t, KC, 256), wsrc(fy, 0, D, nb * 256, 256), fyb)]))
            loads.append((('lo', m, nb), [(lambda t: wview(t, 2, D), wsrc(fo, nb * 256, 256, 0, D), fob)]))
        return loads

    def lru(layer):
        m = layer // 2
        with contextlib.ExitStack() as ph:
            lv = pt(ph, "lru_lv", [128, NLV * KC]); lv_b = Buf()
            dma(lv[:], lruvec_d[m], [], [lv_b])

            def V(kind, c):
                return lv[:, kind * KC + c:kind * KC + c + 1]
            spn = pt(ph, "lru_sp", [128, 2 * KC]); spn_b = Buf()
            act(spn[:, 0:KC], lv[:, 9 * KC:10 * KC], AF.Exp, [lv_b], [spn_b])
            act(spn[:, 0:KC], spn[:, 0:KC], AF.Ln, [spn_b], [spn_b], bias=1.0)
            ts(spn[:, KC:2 * KC], spn[:, 0:KC], -16.0, None, ALU.mult, None, [spn_b], [spn_b])
            ts(spn[:, 0:KC], spn[:, 0:KC], -8.0, None, ALU.mult, None, [spn_b], [spn_b])
            tail = pt(ph, "lru_tail", [128, KC * 3]); tail_b = Buf()
            hist = pt(ph, "lru_hist", [128, KC * 3]); hist_b = Buf()
            tp, tpb = PS[5], PSb[5]
            for nb in range(8):
                si, wt, wb = ws.get(('lt', m, nb))
                w = wview(wt, KC, 256)
                for j in range(2):
                    c = nb * 2 + j
                    for kc in range(KC):
                        mm(tp[:, c * 3:c * 3 + 3], w[:, kc, j * 128:(j + 1) * 128], hT[:, kc, NTP - 3:NTP], kc == 0, kc == KC - 1, [wb, hT_b], [tpb])
                ws.release(si)
            tt(tail[:].rearrange("p (c k) -> p c k", k=3), tp[:, 0:KC * 3].rearrange("p (c k) -> p c k", k=3),
               lv[:, 0:KC].unsqueeze(2).to_broadcast([128, KC, 3]), ALU.add, [tpb, lv_b], [tail_b])
            dma(o_lru_conv_p[m], tail[:], [tail_b], [])
            pair_exchange(tail[:], KC * 3, [tail_b], hist[:], hist_b)
            cin = pt(ph, "lru_cin", [128, NTP + 3]); cin_b = Buf()
            xc = pt(ph, "lru_xc", [128, 2, NT]); xc_b = Buf()
            xcb = pt(ph, "lru_xcb", [128, 2, NT], BF16); xcb_b = Buf()
            aa = pt(ph, "lru_a", [128, 2, NT]); aa_b = Buf()
            bbt = pt(ph, "lru_b", [128, 2, NT]); bb_b = Buf()
            hp = pt(ph, "lru_h", [128, 2, NT]); hp_b = Buf()
            hg = pt(ph, "lru_hg", [128, 2, NT], BF16); hg_b = Buf()
            t1 = pt(ph, "lru_t1", [128, 512]); t1_b = Buf()
            t2 = pt(ph, "lru_t2", [128, 512]); t2_b = Buf()
            t3 = pt(ph, "lru_t3", [128, 512]); t3_b = Buf()
            hs = pt(ph, "lru_hs", [128, 2, 3, NTS]); hs_b = Buf()
            raws = pt(ph, "lru_raws", [128, 2, NTS]); raws_b = Buf()
            h0 = pt(ph, "lru_h0", [128, 2, NTS]); h0_b = Buf()
            fin = pt(ph, "lru_fin", [128, 2]); fin_b = Buf()
            prev = pt(ph, "lru_prev", [128, 2]); prev_b = Buf()
            fin2 = pt(ph, "lru_fin2", [128, 2]); fin2_b = Buf()
            cnt = 0
            for nb in range(8):
                c0 = nb * 2
                dma(hs[:], st_lru_convT_d[m][:, c0:c0 + 2], [], [hs_b])
                dma(h0[:], st_lruT_d[m][:, c0:c0 + 2], [], [h0_b])
                si, wt, wb = ws.get(('lx', m, nb))
                w = wview(wt, KC, 256)
                for j in range(2):
                    c = c0 + j
                    for (t0, n) in TT:
                        ps, psb = PS[cnt % 2], PSb[cnt % 2]
                        cnt += 1
                        for kc in range(KC):
                            mm(ps[:, :n], w[:, kc, j * 128:(j + 1) * 128], hT[:, kc, t0:t0 + n], kc == 0, kc == KC - 1, [wb, hT_b], [psb])
                        if t0 < NTP:
                            act(cin[:, 3 + t0:3 + t0 + n], ps[:, :n], AF.Identity, [psb, lv_b], [cin_b], bias=V(0, c), scale=1.0)
                        else:
                            act(raws[:, j, :], ps[:, :n], AF.Identity, [psb, lv_b], [raws_b], bias=V(0, c), scale=1.0)
                    cp(cin[:, 0:3], hist[:, c * 3:c * 3 + 3], [hist_b], [cin_b])
                    ts(xc[:, j, 0:NTP], cin[:, 0:NTP], V(2, c), V(6, c), ALU.mult, ALU.add, [cin_b, lv_b], [xc_b])
                    for kk in range(1, 4):
                        stt(xc[:, j, 0:NTP], cin[:, kk:kk + NTP], V(2 + kk, c), xc[:, j, 0:NTP], ALU.mult, ALU.add, [cin_b, xc_b, lv_b], [xc_b])
                    ts(xc[:, j, NTP:NT], hs[:, j, 0, :], V(2, c), V(6, c), ALU.mult, ALU.add, [hs_b, lv_b], [xc_b])
                    for kk in range(1, 3):
                        stt(xc[:, j, NTP:NT], hs[:, j, kk, :], V(2 + kk, c), xc[:, j, NTP:NT], ALU.mult, ALU.add, [hs_b, xc_b, lv_b], [xc_b])
                    stt(xc[:, j, NTP:NT], raws[:, j, :], V(5, c), xc[:, j, NTP:NT], ALU.mult, ALU.add, [raws_b, xc_b, lv_b], [xc_b])
                ws.release(si)
                dma(o_lru_conv_sT[m][:, c0:c0 + 2, 0:2, :], hs[:, :, 1:3, :], [hs_b], [])
                dma(o_lru_conv_sT[m][:, c0:c0 + 2, 2, :], raws[:], [raws_b], [])
                act(xcb[:], xc[:], AF.Copy, [xc_b], [xcb_b])
                si, wt, wb = ws.get(('lg', m, nb))
                wg = wview(wt, 4, 256)
                for mo in range(2):
                    c = c0 + mo
                    for (t0, n) in TT:
                        pr, prb = PS[cnt % 2], PSb[cnt % 2]
                        pi, pib = PS[2 + cnt % 2], PSb[2 + cnt % 2]
                        cnt += 1
                        for kc in range(2):
                            mm(pr[:, :n], wg[:, kc, mo * 128:(mo + 1) * 128], xcb[:, kc, t0:t0 + n], kc == 0, kc == 1, [wb, xcb_b], [prb])
                        for kc in range(2):
                            mm(pi[:, :n], wg[:, 2 + kc, mo * 128:(mo + 1) * 128], xcb[:, kc, t0:t0 + n], kc == 0, kc == 1, [wb, xcb_b], [pib])
                        act(t1[:, :n], pr[:, :n], AF.Sigmoid, [prb, lv_b], [t1_b], bias=V(7, c), scale=1.0)
                        act(aa[:, mo, t0:t0 + n], t1[:, :n], AF.Exp, [t1_b, spn_b], [aa_b], scale=spn[:, c:c + 1])
                        act(t2[:, :n], t1[:, :n], AF.Exp, [t1_b, spn_b], [t2_b], scale=spn[:, KC + c:KC + c + 1])
                        act(t2[:, :n], t2[:, :n], AF.Sqrt, [t2_b], [t2_b], scale=-1.0, bias=1.0)
                        act(t3[:, :n], pi[:, :n], AF.Sigmoid, [pib, lv_b], [t3_b], bias=V(8, c), scale=1.0)
                        tt(t2[:, :n], t2[:, :n], t3[:, :n], ALU.mult, [t2_b, t3_b], [t2_b])
                        tt(bbt[:, mo, t0:t0 + n], t2[:, :n], xc[:, mo, t0:t0 + n], ALU.mult, [t2_b, xc_b], [bb_b])
                ws.release(si)
                si, wt, wb = ws.get(('ly', m, nb))
                w = wview(wt, KC, 256)
                for j in range(2):
                    c = c0 + j
                    for (t0, n) in TT:
                        ps, psb = PS[cnt % 2], PSb[cnt % 2]
                        cnt += 1
                        for kc in range(KC):
                            mm(ps[:, :n], w[:, kc, j * 128:(j + 1) * 128], hT[:, kc, t0:t0 + n], kc == 0, kc == KC - 1, [wb, hT_b], [psb])
                        act(hg[:, j, t0:t0 + n], ps[:, :n], AF.Gelu_apprx_tanh, [psb, lv_b], [hg_b], bias=V(1, c), scale=1.0)
                ws.release(si)
                for mo in range(2):
                    k.op('dve', lambda e: e.tensor_tensor_scan(out=hp[:, mo, 0:NTP], data0=aa[:, mo, 0:NTP], data1=bbt[:, mo, 0:NTP], initial=0.0, op0=ALU.mult, op1=ALU.add),
                         reads=[aa_b, bb_b], writes=[hp_b])
                cp(fin[:], hp[:, :, NTP - 1], [hp_b], [fin_b])
                pair_exchange(fin[:], 2, [fin_b], prev[:], prev_b)
                for mo in range(2):
                    k.op('dve', lambda e: e.tensor_tensor_scan(out=hp[:, mo, 0:NTP], data0=aa[:, mo, 0:NTP], data1=bbt[:, mo, 0:NTP], initial=prev[:, mo:mo + 1], op0=ALU.mult, op1=ALU.add),
                         reads=[aa_b, bb_b, prev_b], writes=[hp_b])
                tt(hp[:, :, NTP:NT], aa[:, :, NTP:NT], h0[:], ALU.mult, [aa_b, h0_b], [hp_b])
                tt(hp[:, :, NTP:NT], hp[:, :, NTP:NT], bbt[:, :, NTP:NT], ALU.add, [hp_b, bb_b], [hp_b])
                cp(fin2[:], hp[:, :, NTP - 1], [hp_b], [fin2_b])
                dma(o_lru_p[m][:, c0:c0 + 2], fin2[:], [fin2_b], [])
                dma(o_lru_sT[m][:, c0:c0 + 2, :], hp[:, :, NTP:NT], [hp_b], [])
                for j in range(2):
                    tt(hg[:, j, :], hg[:, j, :], hp[:, j, :], ALU.mult, [hg_b, hp_b], [hg_b])
                si, wt, wb = ws.get(('lo', m, nb))
                wo = wview(wt, 2, D)
                for dc in range(KC):
                    for (t0, n) in TT:
                        po, pob = PS[4 + cnt % 2], PSb[4 + cnt % 2]
                        cnt += 1
                        for rc in range(2):
                            mm(po[:, :n], wo[:, rc, dc * 128:(dc + 1) * 128], hg[:, rc, t0:t0 + n], rc == 0, rc == 1, [wb, hg_b], [pob])
                        if nb == 0:
                            stt(xT[:, dc, t0:t0 + n], po[:, :n], V(10, dc), xT[:, dc, t0:t0 + n], ALU.add, ALU.add, [pob, xT_b[dc], lv_b], [xT_b[dc]])
                        else:
                            tt(xT[:, dc, t0:t0 + n], xT[:, dc, t0:t0 + n], po[:, :n], ALU.add, [pob, xT_b[dc]], [xT_b[dc]])
                ws.release(si)
            k.barrier()

    def mamba_loads(layer):
        m = layer // 2
        fi, fib = wf['ssm_w_in'][m]; fo, fob = wf['ssm_w_out'][m]
        loads = []
        for cg in range(24):
            loads.append((('mt', m, cg), [(lambda t: wview(t, KC, 256), wsrc(fi, 0, D, DI + cg * 256, 256), fib)]))
        loads.append((('mdt', m), [(lambda t: wview(t, KC, 64), wsrc(fi, 0, D, DI + CONVD, 64), fib)]))
        for g in range(NG):
            for j in range(2):
                loads.append((('mx', m, g, j), [(lambda t: wview(t, KC, 256), wsrc(fi, 0, D, DI + g * 512 + j * 256, 256), fib)]))
            loads.append((('mbc', m, g), [(lambda t: wview(t, KC, 256)[:, :, 0:128], wsrc(fi, 0, D, 2 * DI + g * 128, 128), fib),
                                          (lambda t: wview(t, KC, 256)[:, :, 128:256], wsrc(fi, 0, D, 2 * DI + 1024 + g * 128, 128), fib)]))
            for j in range(2):
                loads.append((('mz', m, g, j), [(lambda t: wview(t, KC, 256), wsrc(fi, 0, D, g * 512 + j * 256, 256), fib)]))
            for j in range(2):
                loads.append((('mo', m, g, j), [(lambda t: wview(t, 2, D), wsrc(fo, g * 512 + j * 256, 256, 0, D), fob)]))
        return loads

    def mamba(layer):
        m = layer // 2
        MUL, ADD = ALU.mult, ALU.add
        with contextlib.ExitStack() as ph:
            sv = pt(ph, "m_sv", [128, 432]); sv_b = Buf()
            dma(sv[:], ssmvec_d[m][:, 0:432], [], [sv_b])
            dtb = sv[:, 0:64]; A_bc = sv[:, 64:128]; D_bc = sv[:, 128:192]
            act(A_bc, A_bc, AF.Exp, [sv_b], [sv_b])
            ts(A_bc, A_bc, -1.0, None, MUL, None, [sv_b], [sv_b])

            def CW(c, kk):
                return sv[:, 192 + c * 5 + kk:192 + c * 5 + kk + 1]
            ng = pt(ph, "m_ng", [128, 512]); ng_b = Buf()
            dt_tok = pt(ph, "m_dt", [128, 9, 64]); dt_b = Buf()
            a_tok = pt(ph, "m_a", [128, 9, 64]); a_b = Buf()
            tail = pt(ph, "m_tail", [128, 144]); tail_b = Buf()
            hist = pt(ph, "m_hist", [128, 144]); hist_b = Buf()
            tp, tpb = PS[7], PSb[7]
            for cg in range(24):
                si, wt, wb = ws.get(('mt', m, cg))
                w = wview(wt, KC, 256)
                for j in range(2):
                    c = cg * 2 + j
                    for kc in range(KC):
                        mm(tp[:, c * 3:c * 3 + 3], w[:, kc, j * 128:(j + 1) * 128], hT[:, kc, NTP - 3:NTP], kc == 0, kc == KC - 1, [wb, hT_b], [tpb])
                ws.release(si)
            act(tail[:], tp[:, 0:144], AF.Copy, [tpb], [tail_b])
            dma(o_ssm_conv_p[m], tail[:], [tail_b], [])
            pair_exchange(tail[:], 144, [tail_b], hist[:], hist_b)
            si, wt, wb = ws.get(('mdt', m))
            wdt = wview(wt, KC, 64)
            dtt = pt(ph, "m_dtt", [128, 64]); dtt_b = Buf()
            for ti in range(9):
                t0 = ti * 128; nt = 128 if ti < 8 else NTS
                ps, psb = PS[ti % 2], PSb[ti % 2]
                for kc in range(KC):
                    mm(ps[:nt, 0:64], hT[:, kc, t0:t0 + nt], wdt[:, kc, :], kc == 0, kc == KC - 1, [wb, hT_b], [psb])
                tt(dtt[:nt], ps[:nt, 0:64], dtb[:nt], ADD, [psb, sv_b], [dtt_b])
                act(dtt[:nt], dtt[:nt], AF.Exp, [dtt_b], [dtt_b])
                act(dt_tok[:nt, ti, :], dtt[:nt], AF.Ln, [dtt_b], [dt_b], bias=1.0)
                tt(a_tok[:nt, ti, :], dt_tok[:nt, ti, :], A_bc[:nt], MUL, [dt_b, sv_b], [a_b])
            ws.release(si)
            cd_all = pt(ph, "m_cdall", [128, 8, 64]); cd_b = Buf()
            eac_all = pt(ph, "m_eacall", [128, 8, 64]); eac_b = Buf()
            sdt_all = pt(ph, "m_sdtall", [128, 8, 64]); sdt_b = Buf()
            for ti in range(8):
                pq, pqb = PS[ti % 2], PSb[ti % 2]
                mm(pq[:, 0:64], ones_f[:], a_tok[:, ti, :], True, True, [a_b, ones_b], [pqb])
                mm(pq[:, 64:128], Umat, a_tok[:, ti, :], True, True, [a_b, consts_b], [pqb])
                act(eac_all[:, ti, :], pq[:, 64:128], AF.Copy, [pqb], [eac_b])
                tt(sdt_all[:, ti, :], pq[:, 0:64], eac_all[:, ti, :], ALU.subtract, [pqb, eac_b], [sdt_b])
                act(cd_all[:, ti, :], pq[:, 0:64], AF.Exp, [pqb], [cd_b])
            act(sdt_all[:], sdt_all[:], AF.Exp, [sdt_b], [sdt_b])
            tt(sdt_all[:], sdt_all[:], dt_tok[:, 0:8, :], MUL, [sdt_b, dt_b], [sdt_b])
            act(eac_all[:], eac_all[:], AF.Exp, [eac_b], [eac_b])
            cin = pt(ph, "m_cin", [128, NTP + 3]); cin_b = Buf()
            ctmp = pt(ph, "m_ctmp", [128, NT]); ctmp_b = Buf()
            xct = [pt(ph, f"m_xct{i}", [128, NT], BF16) for i in range(2)]; xct_b = [Buf(), Buf()]
            xbc = pt(ph, "m_xbc", [128, 2, NT], BF16); xbc_b = [Buf(), Buf()]
            x_tok = pt(ph, "m_xtok", [128, 9, 512], BF16); xtok_b = Buf()
            B_tok = pt(ph, "m_btok", [128, 9, 128], BF16); btok_b = Buf()
            C_s = pt(ph, "m_cs", [128, 128], BF16); cs_b = Buf()
            hs = pt(ph, "m_hs", [128, 6, 3, NTS]); hs_b = Buf()
            raws = pt(ph, "m_raws", [128, 6, NTS]); raws_b = Buf()
            hst = pt(ph, "m_h", [128, 512]); hst_b = Buf()
            hbf = pt(ph, "m_hbf", [128, 512], BF16); hbf_b = Buf()
            ynT = pt(ph, "m_ynT", [128, 4, 512], BF16); ynT_b = Buf()
            sm = pt(ph, "m_small", [128, 16]);
            eas = sm[:, 0:8]
            ss = sm[:, 8:9]
            eas_b, ss_b = Buf(), Buf()
            mcbT = pt(ph, "m_mcbT", [128, 128]); mcbT_b = Buf()
            decT = pt(ph, "m_decT", [128, 1024], BF16); decT_b = Buf()
            wT = pt(ph, "m_wT", [128, 1024], BF16); wT_b = Buf()
            negU = consts[:, 384:512]
            xs = pt(ph, "m_xs", [128, 512], BF16); xs_b = Buf()
            xdt = pt(ph, "m_xdt", [128, 512], BF16); xdt_b = Buf()
            f1 = pt(ph, "m_f1", [128, 512]); f1_b = Buf()
            f2 = pt(ph, "m_f2", [128, 512]); f2_b = Buf()
            zs = ctmp[:, 0:512]; zs_b = ctmp_b
            yn = pt(ph, "m_yn", [128, 512], BF16); yn_b = Buf()
            earep = pt(ph, "m_earep", [128, 128]); earep_b = Buf()
            dec = pt(ph, "m_dec", [128, NTS]); dec_b = Buf()
            xdtp, xdtp_b = zs, zs_b
            dtx = pt(ph, "m_dtx", [128, NTS * 4]); dtx_b = Buf()
            bcd = pt(ph, "m_bcd", [128, 512], BF16); bcd_b = Buf()
            st = ctmp[:, 0:1024].rearrange("p (b i n) -> p b i n", b=2, i=4); st_b = ctmp_b
            u2 = cin[:, 0:1024].rearrange("p (b i n) -> p b i n", b=2, i=4); u2_b = cin_b
            bc2, bc2_b = f1, f1_b
            ysp = pt(ph, "m_ysp", [128, NTS, 4]); ysp_b = Buf()
            pstb = PS[2][:].bitcast(BF16)
            pstv = pstb[:, 0:512].rearrange("p (q t) -> p q t", q=4)
            cnt = 0

            def v864(ap):
                return ap.rearrange("p (j q) -> p j q", q=64)

            def bc864(ap, nr=128):
                return ap.unsqueeze(2).to_broadcast([nr, 8, 64])

            for g in range(NG):
                hs8 = slice(8 * g, 8 * g + 8)
                dma(ng[:], ssmvec_d[m][:, 432 + g * 512:432 + (g + 1) * 512], [], [ng_b])
                dma(hs[:, 0:4], st_ssm_convT_d[m][:, 4 * g:4 * g + 4], [], [hs_b])
                dma(hs[:, 4], st_ssm_convT_d[m][:, 32 + g], [], [hs_b])
                dma(hs[:, 5], st_ssm_convT_d[m][:, 40 + g], [], [hs_b])
                si = None
                for ci in range(6):
                    if ci in (0, 2, 4):
                        if si is not None:
                            ws.release(si)
                        key = ('mx', m, g, ci // 2) if ci < 4 else ('mbc', m, g)
                        si, wt, wb = ws.get(key)
                        w = wview(wt, KC, 256)
                    jj = ci % 2
                    c = (4 * g + ci) if ci < 4 else (32 + g if ci == 4 else 40 + g)
                    for (t0, n) in TT:
                        ps, psb = PS[cnt % 2], PSb[cnt % 2]
                        cnt += 1
                        for kc in range(KC):
                            mm(ps[:, :n], w[:, kc, jj * 128:(jj + 1) * 128], hT[:, kc, t0:t0 + n], kc == 0, kc == KC - 1, [wb, hT_b], [psb])
                        if t0 < NTP:
                            act(cin[:, 3 + t0:3 + t0 + n], ps[:, :n], AF.Copy, [psb], [cin_b])
                        else:
                            act(raws[:, ci, :], ps[:, :n], AF.Copy, [psb], [raws_b])
                    cp(cin[:, 0:3], hist[:, c * 3:c * 3 + 3], [hist_b], [cin_b])
                    ts(ctmp[:, 0:NTP], cin[:, 0:NTP], CW(c, 0), CW(c, 4), MUL, ADD, [cin_b, sv_b], [ctmp_b])
                    for kk in range(1, 4):
                        stt(ctmp[:, 0:NTP], cin[:, kk:kk + NTP], CW(c, kk), ctmp[:, 0:NTP], MUL, ADD, [cin_b, sv_b, ctmp_b], [ctmp_b])
                    ts(ctmp[:, NTP:NT], hs[:, ci, 0, :], CW(c, 0), CW(c, 4), MUL, ADD, [hs_b, sv_b], [ctmp_b])
                    for kk in range(1, 3):
                        stt(ctmp[:, NTP:NT], hs[:, ci, kk, :], CW(c, kk), ctmp[:, NTP:NT], MUL, ADD, [hs_b, sv_b, ctmp_b], [ctmp_b])
                    stt(ctmp[:, NTP:NT], raws[:, ci, :], CW(c, 3), ctmp[:, NTP:NT], MUL, ADD, [raws_b, sv_b, ctmp_b], [ctmp_b])
                    if ci < 4:
                        src, srcb = xct[ci % 2][:], xct_b[ci % 2]
                    else:
                        src, srcb = xbc[:, ci - 4, :], xbc_b[ci - 4]
                    act(src, ctmp[:], AF.Silu, [ctmp_b], [srcb])
                    if ci < 5:
                        for half in range(2):
                            for q in range(4):
                                ti = half * 4 + q
                                tr(pstv[:, q, :], src[:, ti * 128:(ti + 1) * 128], ident_bf[:], [srcb, identbf_b], [PSb[2]])
                            if ci < 4:
                                cp(x_tok[:, half * 4:half * 4 + 4, ci * 128:(ci + 1) * 128], pstv, [PSb[2]], [xtok_b], eng='act' if half else 'dve')
                            else:
                                cp(B_tok[:, half * 4:half * 4 + 4, :], pstv, [PSb[2]], [btok_b], eng='act' if half else 'dve')
                    tr(pstb[:NTS, 0:128], src[:, NTP:NT], ident_bf[:], [srcb, identbf_b], [PSb[2]])
                    if ci < 4:
                        cp(x_tok[:NTS, 8, ci * 128:(ci + 1) * 128], pstb[:NTS, 0:128], [PSb[2]], [xtok_b])
                    elif ci == 4:
                        cp(B_tok[:NTS, 8, :], pstb[:NTS, 0:128], [PSb[2]], [btok_b])
                    else:
                        cp(C_s[:NTS, :], pstb[:NTS, 0:128], [PSb[2]], [cs_b])
                ws.release(si)
                for (lo, hi, cc0) in ((0, 4, 4 * g), (4, 5, 32 + g), (5, 6, 40 + g)):
                    dma(o_ssm_conv_sT[m][:, cc0:cc0 + hi - lo, 0:2, :], hs[:, lo:hi, 1:3, :], [hs_b], [])
                    dma(o_ssm_conv_sT[m][:, cc0:cc0 + hi - lo, 2, :], raws[:, lo:hi, :], [raws_b], [])
                wz = []; wzb = []; wo = []; wob = []; held = []
                for j in range(2):
                    s_, t_, b_ = ws.get(('mz', m, g, j)); held.append(s_); wz.append(wview(t_, KC, 256)); wzb.append(b_)
                for j in range(2):
                    s_, t_, b_ = ws.get(('mo', m, g, j)); held.append(s_); wo.append(wview(t_, 2, D)); wob.append(b_)

                def finish_tile(ti, nr):
                    t0 = ti * 128
                    zp, zpb = PS[0], PSb[0]
                    for j in range(2):
                        for kc in range(KC):
                            mm(zp[:nr, j * 256:(j + 1) * 256], hT[:, kc, t0:t0 + nr], wz[j][:, kc, :], kc == 0, kc == KC - 1, [wzb[j], hT_b], [zpb])
                    act(zs[:nr], zp[:nr], AF.Silu, [zpb], [zs_b])
                    tt(f2[:nr], f2[:nr], zs[:nr], MUL, [f2_b, zs_b], [f2_b])
                    act(f1[:nr], f2[:nr], AF.Square, [f2_b], [f1_b, ss_b], accum_out=ss[:nr])
                    act(ss[:nr], ss[:nr], AF.Sqrt, [ss_b], [ss_b], scale=1.0 / 512, bias=EPS)
                    k.op('dve', lambda e: e.reciprocal(out=ss[:nr], in_=ss[:nr]), reads=[ss_b], writes=[ss_b])
                    stt(yn[:nr], f2[:nr], ss[:nr], ng[:nr], MUL, MUL, [f2_b, ss_b, ng_b], [yn_b])
                    for rc in range(4):
                        tr(pstv[:, rc, 0:nr], yn[:nr, rc * 128:(rc + 1) * 128], ident_bf[:nr, :nr], [yn_b, identbf_b], [PSb[2]])
                    if ti < 8:
                        q = ti % 4
                        cp(ynT[:, :, q * 128:(q + 1) * 128], pstv, [PSb[2]], [ynT_b], eng='act')
                    else:
                        cp(ynT[:, :, 0:NTS], pstv[:, :, 0:NTS], [PSb[2]], [ynT_b], eng='act')
                    if ti in (3, 7, 8):
                        ncol = 512 if ti < 8 else NTS
                        x0 = (ti // 4) * 512 if ti < 8 else NTP
                        for dc in range(KC):
                            for rc in range(4):
                                mm(PS[7][:, :ncol], wo[rc // 2][:, rc % 2, dc * 128:(dc + 1) * 128], ynT[:, rc, 0:ncol], rc == 0, rc == 3, [wob[rc // 2], ynT_b], [PSb[7]])
                            tt(xT[:, dc, x0:x0 + ncol], xT[:, dc, x0:x0 + ncol], PS[7][:, :ncol], ADD, [PSb[7], xT_b[dc]], [xT_b[dc]])

                def chunk(ti, with_y):
                    a8 = a_tok[:, ti, hs8]; dt8 = dt_tok[:, ti, hs8]
                    t0 = ti * 128
                    pm, pmb = PS[4], PSb[4]
                    cd = cd_all[:, ti, hs8]; sdt = sdt_all[:, ti, hs8]; eac = eac_all[:, ti, hs8]
                    xv = v864(x_tok[:, ti, :])
                    if with_y:
                        BT = xbc[:, 0, t0:t0 + 128]; CT = xbc[:, 1, t0:t0 + 128]
                        mm(pm[:, 128:256], BT, CT, True, True, [xbc_b[0], xbc_b[1]], [pmb])
                        tt(mcbT[:], pm[:, 128:256], Umat, MUL, [pmb, consts_b], [mcbT_b])
                        tt(v864(xdt[:]), xv, bc864(dt8), MUL, [xtok_b, dt_b], [xdt_b])
                        yps, ypsb = PS[1], PSb[1]
                        arep3 = cin[:, 0:1024].rearrange("p (j q) -> p j q", j=8)
                        arepU3 = ctmp[:, 0:1024].rearrange("p (j q) -> p j q", j=8)
                        cp(arep3, a8.unsqueeze(2).to_broadcast([128, 8, 128]), [a_b], [cin_b])
                        tt(arepU3, arep3, negU.unsqueeze(1).to_broadcast([128, 8, 128]), MUL, [cin_b, consts_b], [ctmp_b])
                        for j in range(8):
                            pa = PS[5 + j // 4][:, (j % 4) * 128:(j % 4 + 1) * 128]; pab = PSb[5 + j // 4]
                            mm(pa, arep3[:, j, :], Umat, True, False, [cin_b, consts_b], [pab])
                            mm(pa, arepU3[:, j, :], ones_f[:], False, True, [ctmp_b, ones_b], [pab])
                        for hh in range(2):
                            act(decT[:, hh * 512:(hh + 1) * 512], PS[5 + hh][:], AF.Exp, [PSb[5 + hh]], [decT_b])
                        stt(wT[:].rearrange("p (j q) -> p j q", j=8), decT[:].rearrange("p (j q) -> p j q", j=8), 1.0,
                            mcbT[:].unsqueeze(1).to_broadcast([128, 8, 128]), ALU.min, MUL, [decT_b, mcbT_b], [wT_b])
                        for j in range(8):
                            mm(yps[:, j * 64:(j + 1) * 64], wT[:, j * 128:(j + 1) * 128], xdt[:, j * 64:(j + 1) * 64], True, True, [wT_b, xdt_b], [ypsb])
                        mm(PS[3][:], CT, hbf[:], True, True, [xbc_b[1], hbf_b], [PSb[3]])
                        tt(v864(f1[:]), v864(PS[3][:]), bc864(eac), MUL, [PSb[3], eac_b], [f1_b])
                        tt(f2[:], f1[:], yps[:], ADD, [f1_b, ypsb], [f2_b])
                        tt(v864(f1[:]), xv, bc864(D_bc[:, hs8]), MUL, [xtok_b, sv_b], [f1_b])
                        tt(f2[:], f2[:], f1[:], ADD, [f2_b, f1_b], [f2_b])
                        finish_tile(ti, 128)
                    tt(v864(xs[:]), xv, bc864(sdt), MUL, [xtok_b, sdt_b], [xs_b])
                    mm(PS[3][:], B_tok[:, ti, :], xs[:], True, True, [btok_b, xs_b], [PSb[3]])
                    tt(v864(f1[:]), v864(hst[:]), bc864(cd), MUL, [hst_b, cd_b], [f1_b])
                    tt(hst[:], f1[:], PS[3][:], ADD, [f1_b, PSb[3]], [hst_b])
                    if with_y:
                        act(hbf[:], hst[:], AF.Copy, [hst_b], [hbf_b])

                k.op('dve', lambda e: e.memset(hst[:], 0.0), writes=[hst_b])
                for ti in range(8):
                    chunk(ti, False)
                pair_exchange(hst[:], 512, [hst_b], hst[:], hst_b)
                act(hbf[:], hst[:], AF.Copy, [hst_b], [hbf_b])
                for ti in range(8):
                    chunk(ti, True)
                dma(o_ssm_p[m, g], hst[:], [hst_b], [])
                a_s = a_tok[:NTS, 8, hs8]; dt_s = dt_tok[:NTS, 8, hs8]
                act(eas[:NTS], a_s, AF.Exp, [a_b], [eas_b])
                cp(earep[:NTS].rearrange("p (j q) -> p j q", q=16), eas[:NTS].unsqueeze(2).to_broadcast([NTS, 8, 16]), [eas_b], [earep_b])
                mm(PS[5][:, 0:NTS], earep[:NTS, :], ident[0:NTS, 0:NTS], True, True, [earep_b, consts_b], [PSb[5]])
                cp(dec[:], PS[5][:, 0:NTS], [PSb[5]], [dec_b])
                tt(v864(f1[:NTS]), v864(x_tok[:NTS, 8, :]), bc864(dt_s, NTS), MUL, [xtok_b, dt_b], [f1_b])
                cp(xdtp[:NTS].rearrange("p (i q) -> p i q", i=4), f1[:NTS].rearrange("p (q i) -> p i q", i=4), [f1_b], [xdtp_b])
                for i in range(4):
                    mm(PS[6][:, i * NTS:(i + 1) * NTS], xdtp[:NTS, i * 128:(i + 1) * 128], ident[0:NTS, 0:NTS], True, True, [xdtp_b, consts_b], [PSb[6]])
                cp(dtx[:].rearrange("p (b i) -> p b i", i=4), PS[6][:, 0:4 * NTS].rearrange("p (i b) -> p b i", i=4), [PSb[6]], [dtx_b])
                for bb in range(NTS // 2):
                    b0 = bb * 2
                    srcv = st_ssm_d[m][b0:b0 + 2, hs8].rearrange("b j p n -> b (j p) n").rearrange("b (q i) n -> q b i n", i=4)
                    dstv = o_ssm_s[m][b0:b0 + 2, hs8].rearrange("b j p n -> b (j p) n").rearrange("b (q i) n -> q b i n", i=4)
                    dma(st, srcv, [], [st_b])
                    I2b = ident[0:NTS, b0:b0 + 2].unsqueeze(2).to_broadcast([NTS, 2, 128])
                    tt(bcd[:NTS, 0:256].rearrange("p (b n) -> p b n", n=128), B_tok[:NTS, 8, :].unsqueeze(1).to_broadcast([NTS, 2, 128]), I2b, MUL, [btok_b, consts_b], [bcd_b])
                    tt(bcd[:NTS, 256:512].rearrange("p (b n) -> p b n", n=128), C_s[:NTS, :].unsqueeze(1).to_broadcast([NTS, 2, 128]), I2b, MUL, [cs_b, consts_b], [bcd_b])
                    mm(PS[5][:, 0:512], ones_bf[0:NTS, :], bcd[:NTS, :], True, True, [bcd_b, ones_b], [PSb[5]])
                    act(bc2[:], PS[5][:], AF.Copy, [PSb[5]], [bc2_b])
                    tt(u2, dtx[:, b0 * 4:(b0 + 2) * 4].rearrange("p (b i) -> p b i", i=4).unsqueeze(3).to_broadcast([128, 2, 4, 128]),
                       bc2[:, 0:256].rearrange("p (b n) -> p b n", n=128).unsqueeze(2).to_broadcast([128, 2, 4, 128]), MUL, [dtx_b, bc2_b], [u2_b])
                    for bi in range(2):
                        stt(st[:, bi], st[:, bi], dec[:, b0 + bi:b0 + bi + 1], u2[:, bi], MUL, ADD, [st_b, dec_b, u2_b], [st_b])
                    dma(dstv, st, [st_b], [])
                    tt(u2, st, bc2[:, 256:512].rearrange("p (b n) -> p b n", n=128).unsqueeze(2).to_broadcast([128, 2, 4, 128]), MUL, [st_b, bc2_b], [u2_b])
                    k.op('dve', lambda e: e.tensor_reduce(out=ysp[:, b0:b0 + 2, :], in_=u2, axis=AX.X, op=ADD), reads=[u2_b], writes=[ysp_b])
                for i in range(4):
                    tr(PS[6][:NTS, i * 128:(i + 1) * 128], ysp[:, :, i], ident, [ysp_b, consts_b], [PSb[6]])
                cp(f2[:NTS].rearrange("p (q i) -> p i q", i=4), PS[6][:NTS, 0:512].rearrange("p (i q) -> p i q", i=4), [PSb[6]], [f2_b])
                tt(v864(f1[:NTS]), v864(x_tok[:NTS, 8, :]), bc864(D_bc[:NTS, hs8], NTS), MUL, [xtok_b, sv_b], [f1_b])
                tt(f2[:NTS], f2[:NTS], f1[:NTS], ADD, [f2_b, f1_b], [f2_b])
                finish_tile(8, NTS)
                for s_ in held:
                    ws.release(s_)
            k.barrier()


    def layer_loads(layer):
        loads = []
        if CFG['ffn']:
            loads += ffn_loads(layer, 0)
        if CFG['mixer']:
            loads += (mamba_loads(layer) if layer % 2 == 0 else lru_loads(layer))
        if CFG['ffn']:
            loads += ffn_loads(layer, 1)
        if CFG['ple']:
            loads += ple_loads(layer)
        return loads

    gather_layer(0)
    ws.plan_loads(layer_loads(0))
    for layer in range(L):
        gather_layer(layer + 1)
        if layer + 1 < L:
            ws.plan_loads(layer_loads(layer + 1))
        if CFG['ffn']:
            rmsnorm(0, layer, hT, hT_b)
            ffn(layer, 0)
        if CFG['mixer']:
            rmsnorm(1, layer, hT, hT_b)
            if layer % 2 == 0:
                mamba(layer)
            else:
                lru(layer)
        if CFG['ffn']:
            rmsnorm(2, layer, hT, hT_b)
            ffn(layer, 1)
        if CFG['ple']:
            rmsnorm(3, layer, hT, hT_b)
            ple(layer)

    rmsnorm(4, 0, xT, xT_b)
    for c in range(KC):
        dma(yT_d[c * 128:(c + 1) * 128, :], xT[:, c, :], [xT_b[c]], [])
    k.finish()
    es.close()
    return nc, k


def build_program():
    nc1, k1 = build(None)
    plan = k1.resolve()
    nc2, k2 = build(plan)
    for e in ENGS:
        assert k1.nops[e] == k2.nops[e]
    return nc2


def _consts():
    c = np.zeros((128, 512), np.float32)
    c[:, 0:128] = np.eye(128, dtype=np.float32)
    c[:, 128:256] = np.triu(np.ones((128, 128), np.float32))
    c[:, 384:512] = -np.triu(np.ones((128, 128), np.float32))
    return c


def _fm(v):
    v = np.asarray(v, np.float32)
    return np.ascontiguousarray(v.reshape(-1, 128).T)


def kernel(**inp):
    f32 = np.float32
    nc = build_program()
    A = {n: np.asarray(v) for n, v in inp.items()}
    x_prompt = A['x_prompt']; x_sample = A['x_sample']; p_prompt = A['p_prompt']; p_sample = A['p_sample']
    gl = [A['norm_ffn_pre'], A['norm_mix'], A['norm_ffn_post'], A['norm_ple']]
    gains = np.zeros((128, (4 * DEPTH + 1) * KC), f32)
    for kind in range(4):
        for l in range(DEPTH):
            j = (kind * DEPTH + l) * KC
            gains[:, j:j + KC] = _fm(gl[kind][l])
    gains[:, 4 * DEPTH * KC:] = _fm(A['norm_final'])
    ssmvec = np.zeros((2, 128, NSV), f32)
    lruvec = np.zeros((2, 128, NLV * KC), f32)
    for m in range(2):
        ssmvec[m, :, 0:64] = A['ssm_dt_bias'][m][None, :]
        ssmvec[m, :, 64:128] = A['ssm_a_log'][m][None, :]
        ssmvec[m, :, 128:192] = A['ssm_d'][m][None, :]
        cw = np.concatenate([A['ssm_conv_w'][m], A['ssm_conv_b'][m][None, :]], 0)
        ssmvec[m, :, 192:192 + 240] = cw.reshape(5, 48, 128).transpose(2, 1, 0).reshape(128, 240)
        ssmvec[m, :, 432:] = A['ssm_norm'][m][None, :]
        vs = [A['lru_b_x'][m], A['lru_b_y'][m], A['lru_conv_w'][m][0], A['lru_conv_w'][m][1], A['lru_conv_w'][m][2], A['lru_conv_w'][m][3],
              A['lru_conv_b'][m], A['lru_b_rgate'][m], A['lru_b_igate'][m], A['lru_a_param'][m], A['lru_b_out'][m]]
        for i, v in enumerate(vs):
            lruvec[m, :, i * KC:(i + 1) * KC] = _fm(v)
    in_maps = []
    for c in range(8):
        s, hb = c // 2, c % 2
        xt = np.concatenate([x_prompt[s, hb * NTP:(hb + 1) * NTP, :], x_sample[c * NTS:(c + 1) * NTS, 0, :]], 0)
        pt_ = np.concatenate([p_prompt[:, s, hb * NTP:(hb + 1) * NTP, :], p_sample[:, c * NTS:(c + 1) * NTS, 0, :]], 1)
        cs = _consts()
        cs[:, 256] = float(hb)
        bs = slice(c * NTS, (c + 1) * NTS)
        m = {"xT": np.ascontiguousarray(xt.T), "pT": np.ascontiguousarray(pt_.transpose(0, 2, 1)), "consts": cs, "gains": gains,
             "ssmvec": ssmvec, "lruvec": lruvec,
             "st_ssm": np.ascontiguousarray(A['state_ssm'][:, bs]),
             "st_ssm_convT": np.ascontiguousarray(A['state_ssm_conv'][:, bs].reshape(2, NTS, 3, 48, 128).transpose(0, 4, 3, 2, 1)),
             "st_lruT": np.ascontiguousarray(A['state_lru'][:, bs].reshape(2, NTS, KC, 128).transpose(0, 3, 2, 1)),
             "st_lru_convT": np.ascontiguousarray(A['state_lru_conv'][:, bs].reshape(2, NTS, 3, KC, 128).transpose(0, 4, 3, 2, 1))}
        for n, (nl, rows, cols) in WSPEC.items():
            m[n + "_full"] = A[n].reshape(nl, rows, cols)
        in_maps.append(m)
    res = run_bass_kernel_spmd(nc, in_maps, core_ids=list(range(8)))
    R = [{n: np.asarray(v) for n, v in r.items()} for r in res.results]
    y_prompt = np.zeros((4, 2048, D), f32); y_sample = np.zeros((128, 1, D), f32)
    ssm_p = np.zeros((2, 4, NH, 64, NST), f32); ssm_conv_p = np.zeros((2, 4, 3, CONVD), f32)
    lru_p = np.zeros((2, 4, D), f32); lru_conv_p = np.zeros((2, 4, 3, D), f32)
    ssm_s = np.zeros((2, 128, NH, 64, NST), f32); ssm_conv_s = np.zeros((2, 128, 3, CONVD), f32)
    lru_s = np.zeros((2, 128, D), f32); lru_conv_s = np.zeros((2, 128, 3, D), f32)
    for c in range(8):
        s, hb = c // 2, c % 2
        r = R[c]
        bs = slice(c * NTS, (c + 1) * NTS)
        yt = r["yT"].T
        y_prompt[s, hb * NTP:(hb + 1) * NTP] = yt[:NTP]
        y_sample[bs, 0] = yt[NTP:]
        ssm_s[:, bs] = r["o_ssm_s"]
        ssm_conv_s[:, bs] = r["o_ssm_conv_sT"].transpose(0, 4, 3, 2, 1).reshape(2, NTS, 3, CONVD)
        lru_s[:, bs] = r["o_lru_sT"].transpose(0, 3, 2, 1).reshape(2, NTS, D)
        lru_conv_s[:, bs] = r["o_lru_conv_sT"].transpose(0, 4, 3, 2, 1).reshape(2, NTS, 3, D)
        if hb == 1:
            ssm_p[:, s] = r["o_ssm_p"].reshape(2, NG, NST, 8, 64).transpose(0, 1, 3, 4, 2).reshape(2, NH, 64, NST)
            ssm_conv_p[:, s] = r["o_ssm_conv_p"].reshape(2, 128, 48, 3).transpose(0, 3, 2, 1).reshape(2, 3, CONVD)
            lru_p[:, s] = r["o_lru_p"].transpose(0, 2, 1).reshape(2, D)
            lru_conv_p[:, s] = r["o_lru_conv_p"].reshape(2, 128, KC, 3).transpose(0, 3, 2, 1).reshape(2, 3, D)
    return (y_prompt, y_sample, ssm_p, ssm_conv_p, lru_p, lru_conv_p, ssm_s, ssm_conv_s, lru_s, lru_conv_s)
```
